# Optimizing a Trainium2 kernel written in Bass

```python
import math
import jax, jax.numpy as jnp
from jax import lax
import numpy as np

D_MODEL = 4096
BATCH = 2
SEQ = 4096
DEPTH = 2

CHUNK = 64
Q_BLOCK = 128
EPS = 1e-6
N_BRANCH = 3
BRANCH_W = 2048

RET_HEADS = 8
RET_DK = 256
RET_DV = BRANCH_W // RET_HEADS
ROPE_BASE = 10000.0

GDN_HEADS = 16
GDN_DK = 128
GDN_DV = BRANCH_W // GDN_HEADS
CONV_W = 4
GDN_CONV_CH = 2 * GDN_HEADS * GDN_DK + GDN_HEADS * GDN_DV

DIFF_HEADS = 8
DIFF_DK = 128
DIFF_DV = BRANCH_W // DIFF_HEADS

NUM_BUCKETS = 32
MAX_DISTANCE = 128

IN_SIZES = (
    RET_HEADS * RET_DK, RET_HEADS * RET_DK, BRANCH_W, BRANCH_W,
    GDN_HEADS * GDN_DK, GDN_HEADS * GDN_DK, BRANCH_W, BRANCH_W, GDN_HEADS, GDN_HEADS,
    DIFF_HEADS * 2 * DIFF_DK, DIFF_HEADS * 2 * DIFF_DK, BRANCH_W, BRANCH_W,
    N_BRANCH * D_MODEL,
)
IN_COLS = sum(IN_SIZES)

kernel_name = 'hybrid_retention_gdn_diffattn_gated_block'


def rmsnorm(x, gain):
    xf = x.astype(jnp.float32)
    y = xf * lax.rsqrt(jnp.mean(xf * xf, axis=-1, keepdims=True) + EPS)
    return (y * gain.astype(jnp.float32)).astype(x.dtype)


def l2norm(x):
    xf = x.astype(jnp.float32)
    return xf * lax.rsqrt(jnp.sum(xf * xf, axis=-1, keepdims=True) + EPS)


def rotary(x, pos):
    half = x.shape[-1] // 2
    inv = ROPE_BASE ** (-jnp.arange(half, dtype=jnp.float32) / half)
    ang = pos.astype(jnp.float32)[:, None] * inv[None, :]
    cos = jnp.cos(ang)[None, :, None, :]
    sin = jnp.sin(ang)[None, :, None, :]
    x1, x2 = x[..., :half], x[..., half:]
    return jnp.concatenate([x1 * cos - x2 * sin, x1 * sin + x2 * cos], axis=-1)


def to_chunks(x):
    b, s, h, d = x.shape
    return x.reshape(b, s // CHUNK, CHUNK, h, d).transpose(1, 0, 3, 2, 4)


def from_chunks(x):
    nc, b, h, c, d = x.shape
    return x.transpose(1, 0, 3, 2, 4).reshape(b, nc * c, h, d)


def retention_branch(q, k, v, z, gn_gain):
    b, s, _ = q.shape
    pos = jnp.arange(s)
    qf = rotary(q.astype(jnp.float32).reshape(b, s, RET_HEADS, RET_DK), pos)
    kf = rotary(k.astype(jnp.float32).reshape(b, s, RET_HEADS, RET_DK), pos) * (RET_DK ** -0.5)
    vf = v.astype(jnp.float32).reshape(b, s, RET_HEADS, RET_DV)
    qc, kc, vc = to_chunks(qf), to_chunks(kf), to_chunks(vf)
    log_gamma = jnp.log(1.0 - 2.0 ** (-5.0 - jnp.arange(RET_HEADS, dtype=jnp.float32)))
    idx = jnp.arange(CHUNK, dtype=jnp.float32)
    rel = idx[:, None] - idx[None, :]
    decay = jnp.where(rel >= 0, jnp.exp(jnp.maximum(rel, 0.0)[None] * log_gamma[:, None, None]), 0.0)
    scores = jnp.einsum('nbhqd,nbhkd->nbhqk', qc, kc) * decay
    intra = jnp.einsum('nbhqk,nbhkv->nbhqv', scores, vc)
    q_decay = jnp.exp((idx + 1.0)[None, :] * log_gamma[:, None])
    k_decay = jnp.exp((CHUNK - 1.0 - idx)[None, :] * log_gamma[:, None])
    chunk_decay = jnp.exp(CHUNK * log_gamma)

    def step(state, inp):
        q_i, k_i, v_i = inp
        inter = jnp.einsum('bhqd,bhdv->bhqv', q_i, state) * q_decay[:, :, None]
        state = state * chunk_decay[:, None, None] + jnp.einsum('bhkd,bhkv->bhdv', k_i * k_decay[:, :, None], v_i)
        return state, inter

    state0 = jnp.zeros((b, RET_HEADS, RET_DK, RET_DV), jnp.float32)
    _, inter = lax.scan(step, state0, (qc, kc, vc))
    o = from_chunks(intra + inter)
    mu = jnp.mean(o, axis=-1, keepdims=True)
    var = jnp.mean(jnp.square(o - mu), axis=-1, keepdims=True)
    o = ((o - mu) * lax.rsqrt(var + EPS)).reshape(b, s, BRANCH_W) * gn_gain.astype(jnp.float32)
    return (o * jax.nn.silu(z.astype(jnp.float32))).astype(q.dtype)


def causal_conv(x, w):
    return lax.conv_general_dilated(
        x, w[:, None, :].astype(x.dtype), window_strides=(1,), padding=[(CONV_W - 1, 0)],
        dimension_numbers=('NWC', 'WIO', 'NWC'), feature_group_count=x.shape[-1])


def gdn_branch(q, k, v, z, a, b_logit, conv_w, a_log, dt_bias, norm_gain):
    bsz, s, _ = q.shape
    qkv = jax.nn.silu(causal_conv(jnp.concatenate([q, k, v], axis=-1), conv_w))
    q, k, v = jnp.split(qkv, [GDN_HEADS * GDN_DK, 2 * GDN_HEADS * GDN_DK], axis=-1)
    qf = l2norm(q.reshape(bsz, s, GDN_HEADS, GDN_DK)) * (GDN_DK ** -0.5)
    kf = l2norm(k.reshape(bsz, s, GDN_HEADS, GDN_DK))
    vf = v.astype(jnp.float32).reshape(bsz, s, GDN_HEADS, GDN_DV)
    beta = jax.nn.sigmoid(b_logit.astype(jnp.float32))
    g = -jnp.exp(a_log.astype(jnp.float32)) * jax.nn.softplus(a.astype(jnp.float32) + dt_bias.astype(jnp.float32))
    nc = s // CHUNK
    qc, kc, vc = to_chunks(qf), to_chunks(kf), to_chunks(vf)
    betac = beta.reshape(bsz, nc, CHUNK, GDN_HEADS).transpose(1, 0, 3, 2)
    gc = jnp.cumsum(g.reshape(bsz, nc, CHUNK, GDN_HEADS).transpose(1, 0, 3, 2), axis=-1)
    idx = jnp.arange(CHUNK)
    tril_incl = idx[:, None] >= idx[None, :]
    strict = idx[:, None] > idx[None, :]
    decay = jnp.exp(jnp.where(tril_incl, gc[..., :, None] - gc[..., None, :], -jnp.inf))
    k_beta = kc * betac[..., None]
    v_beta = vc * betac[..., None]
    a_mat = jnp.where(strict, jnp.einsum('nbhid,nbhjd->nbhij', k_beta, kc) * decay, 0.0)
    lower = a_mat + jnp.eye(CHUNK, dtype=jnp.float32)
    rhs = jnp.concatenate([v_beta, k_beta * jnp.exp(gc)[..., None]], axis=-1)
    sol = lax.linalg.triangular_solve(lower, rhs, left_side=True, lower=True, unit_diagonal=True)
    u, w = sol[..., :GDN_DV], sol[..., GDN_DV:]
    attn = jnp.einsum('nbhid,nbhjd->nbhij', qc, kc) * decay
    q_g = qc * jnp.exp(gc)[..., None]
    g_last = gc[..., -1]
    k_g = kc * jnp.exp(g_last[..., None] - gc)[..., None]

    def step(state, inp):
        u_i, w_i, attn_i, q_i, k_i, gl = inp
        v_new = u_i - jnp.einsum('bhck,bhkv->bhcv', w_i, state)
        o = jnp.einsum('bhck,bhkv->bhcv', q_i, state) + jnp.einsum('bhij,bhjv->bhiv', attn_i, v_new)
        state = state * jnp.exp(gl)[..., None, None] + jnp.einsum('bhck,bhcv->bhkv', k_i, v_new)
        return state, o

    state0 = jnp.zeros((bsz, GDN_HEADS, GDN_DK, GDN_DV), jnp.float32)
    _, o = lax.scan(step, state0, (u, w, attn, q_g, k_g, g_last))
    o = from_chunks(o)
    o = o * lax.rsqrt(jnp.mean(o * o, axis=-1, keepdims=True) + EPS) * norm_gain.astype(jnp.float32)
    o = o.reshape(bsz, s, BRANCH_W) * jax.nn.silu(z.astype(jnp.float32))
    return o.astype(q.dtype)


def rel_bucket(rel):
    nb = NUM_BUCKETS // 2
    max_exact = nb // 2
    base = jnp.where(rel > 0, nb, 0)
    n = jnp.abs(rel)
    nf = jnp.maximum(n, 1).astype(jnp.float32)
    large = max_exact + (jnp.log(nf / max_exact) / math.log(MAX_DISTANCE / max_exact) * (nb - max_exact)).astype(jnp.int32)
    large = jnp.minimum(large, nb - 1)
    return base + jnp.where(n < max_exact, n, large)


def diff_branch(q, k, v, z, q_gain, k_gain, lq1, lk1, lq2, lk2, subln_gain, rel_bias, layer_idx):
    bsz, s, _ = q.shape
    lam_init = 0.8 - 0.6 * math.exp(-0.3 * layer_idx)
    lam = (jnp.exp(jnp.sum(lq1.astype(jnp.float32) * lk1.astype(jnp.float32)))
           - jnp.exp(jnp.sum(lq2.astype(jnp.float32) * lk2.astype(jnp.float32))) + lam_init)
    qn = rmsnorm(q.reshape(bsz, s, DIFF_HEADS, 2, DIFF_DK), q_gain) * (DIFF_DK ** -0.5)
    kn = rmsnorm(k.reshape(bsz, s, DIFF_HEADS, 2, DIFF_DK), k_gain)
    vh = v.reshape(bsz, s, DIFF_HEADS, DIFF_DV)
    nb = s // Q_BLOCK
    qb = qn.reshape(bsz, nb, Q_BLOCK, DIFF_HEADS, 2, DIFF_DK).transpose(1, 0, 3, 4, 2, 5)
    kt = kn.transpose(0, 2, 3, 1, 4)
    vt = vh.transpose(0, 2, 1, 3)
    k_pos = jnp.arange(s)

    def block(inp):
        q_i, blk = inp
        q_pos = blk * Q_BLOCK + jnp.arange(Q_BLOCK)
        bias = rel_bias[rel_bucket(k_pos[None, :] - q_pos[:, None])]
        bias = bias.astype(jnp.float32).transpose(2, 0, 1)
        logits = jnp.einsum('bhmqd,bhmkd->bhmqk', q_i, kt).astype(jnp.float32) + bias[None, :, None]
        visible = (k_pos[None, :] // CHUNK) <= (q_pos[:, None] // CHUNK)
        p = jax.nn.softmax(jnp.where(visible, logits, -jnp.inf), axis=-1)
        attn = p[:, :, 0] - lam * p[:, :, 1]
        return jnp.einsum('bhqk,bhkv->bhqv', attn.astype(vt.dtype), vt)

    o = lax.map(block, (qb, jnp.arange(nb)))
    o = o.transpose(1, 0, 3, 2, 4).reshape(bsz, s, DIFF_HEADS, DIFF_DV)
    o = rmsnorm(o, subln_gain).astype(jnp.float32) * (1.0 - lam_init)
    o = o.reshape(bsz, s, BRANCH_W) * jax.nn.silu(z.astype(jnp.float32))
    return o.astype(q.dtype)


def setup_inputs(seed: int = 0) -> dict:
    key = jax.random.key(seed)
    ks = jax.random.split(key, 20)
    f32 = jnp.float32
    x = jax.random.normal(ks[0], (BATCH, SEQ, D_MODEL), f32)
    norm_gain = 1.0 + 0.02 * jax.random.normal(ks[1], (DEPTH, D_MODEL), f32)
    w_in = jax.random.normal(ks[2], (DEPTH, D_MODEL, IN_COLS), f32) * (D_MODEL ** -0.5)
    ret_gn_gain = 1.0 + 0.02 * jax.random.normal(ks[3], (DEPTH, BRANCH_W), f32)
    gdn_conv_w = jax.random.normal(ks[4], (DEPTH, CONV_W, GDN_CONV_CH), f32) * (CONV_W ** -0.5)
    gdn_a_log = jnp.log(jax.random.uniform(ks[5], (DEPTH, GDN_HEADS), f32, 1.0, 16.0))
    dt = jnp.exp(jax.random.uniform(ks[6], (DEPTH, GDN_HEADS), f32, math.log(1e-3), math.log(1e-1)))
    gdn_dt_bias = dt + jnp.log(-jnp.expm1(-dt))
    gdn_norm_gain = 1.0 + 0.02 * jax.random.normal(ks[7], (DEPTH, GDN_DV), f32)
    diff_q_gain = 1.0 + 0.02 * jax.random.normal(ks[8], (DEPTH, DIFF_DK), f32)
    diff_k_gain = 1.0 + 0.02 * jax.random.normal(ks[9], (DEPTH, DIFF_DK), f32)
    diff_lambda_q1 = 0.1 * jax.random.normal(ks[10], (DEPTH, DIFF_DK), f32)
    diff_lambda_k1 = 0.1 * jax.random.normal(ks[11], (DEPTH, DIFF_DK), f32)
    diff_lambda_q2 = 0.1 * jax.random.normal(ks[12], (DEPTH, DIFF_DK), f32)
    diff_lambda_k2 = 0.1 * jax.random.normal(ks[13], (DEPTH, DIFF_DK), f32)
    diff_subln_gain = 1.0 + 0.02 * jax.random.normal(ks[14], (DEPTH, DIFF_DV), f32)
    rel_bias = 0.5 * jax.random.normal(ks[15], (NUM_BUCKETS, DIFF_HEADS), f32)
    w_branch = jax.random.normal(ks[16], (DEPTH, N_BRANCH, BRANCH_W, D_MODEL), f32) * (BRANCH_W ** -0.5)
    w_out = jax.random.normal(ks[17], (DEPTH, D_MODEL, D_MODEL), f32) * (D_MODEL ** -0.5)
    return {'x': x, 'norm_gain': norm_gain, 'w_in': w_in, 'ret_gn_gain': ret_gn_gain,
            'gdn_conv_w': gdn_conv_w, 'gdn_a_log': gdn_a_log, 'gdn_dt_bias': gdn_dt_bias,
            'gdn_norm_gain': gdn_norm_gain, 'diff_q_gain': diff_q_gain, 'diff_k_gain': diff_k_gain,
            'diff_lambda_q1': diff_lambda_q1, 'diff_lambda_k1': diff_lambda_k1,
            'diff_lambda_q2': diff_lambda_q2, 'diff_lambda_k2': diff_lambda_k2,
            'diff_subln_gain': diff_subln_gain, 'rel_bias': rel_bias,
            'w_branch': w_branch, 'w_out': w_out}


def reference(x, norm_gain, w_in, ret_gn_gain, gdn_conv_w, gdn_a_log, gdn_dt_bias, gdn_norm_gain,
              diff_q_gain, diff_k_gain, diff_lambda_q1, diff_lambda_k1, diff_lambda_q2, diff_lambda_k2,
              diff_subln_gain, rel_bias, w_branch, w_out):
    split_points = [int(c) for c in np.cumsum(IN_SIZES)[:-1]]
    bsz, s, _ = x.shape
    for l in range(DEPTH):
        h = rmsnorm(x, norm_gain[l])
        p = jnp.einsum('bsd,dc->bsc', h, w_in[l])
        (rq, rk, rv, rz, gq, gk, gv, gz, ga, gb, dq, dk, dv, dz, gate) = jnp.split(p, split_points, axis=-1)
        y_ret = retention_branch(rq, rk, rv, rz, ret_gn_gain[l])
        y_gdn = gdn_branch(gq, gk, gv, gz, ga, gb, gdn_conv_w[l], gdn_a_log[l], gdn_dt_bias[l], gdn_norm_gain[l])
        y_diff = diff_branch(dq, dk, dv, dz, diff_q_gain[l], diff_k_gain[l], diff_lambda_q1[l], diff_lambda_k1[l],
                             diff_lambda_q2[l], diff_lambda_k2[l], diff_subln_gain[l], rel_bias, l)
        y = jnp.stack([y_ret, y_gdn, y_diff], axis=2)
        branch = jnp.einsum('bsnw,nwd->bsnd', y, w_branch[l])
        gates = jax.nn.sigmoid(gate.reshape(bsz, s, N_BRANCH, D_MODEL))
        merged = jnp.sum(gates * branch, axis=2)
        x = x + jnp.einsum('bsd,de->bse', merged, w_out[l])
    return x
```

```python
import numpy as np
import concourse.bass as bass
import concourse.mybir as mybir

F32 = mybir.dt.float32
BF16 = mybir.dt.bfloat16
AF = mybir.ActivationFunctionType
ALU = mybir.AluOpType
AX = mybir.AxisListType


class Buf:
    __slots__ = ("t", "name", "w", "r", "dsem", "dcount", "root", "excl", "persist")

    def __init__(self, t, name, root=None, excl=False):
        self.t = t
        self.name = name
        self.root = root if root is not None else self
        self.excl = excl
        self.persist = False
        self.w = {}
        self.r = {}
        self.dsem = None
        self.dcount = 0

    def __getitem__(self, idx):
        return self.t[idx]

    def view(self, ap, name):
        return Buf(ap, name, root=self.root)


class KB:
    def __init__(self, nc, same_engine_sync=True):
        self.nc = nc
        self.stack = []
        self.engs = {}
        self.same_engine_sync = same_engine_sync
        for name, e in (("pe", nc.tensor), ("act", nc.scalar), ("dve", nc.vector),
                        ("pool", nc.gpsimd), ("sp", nc.sync)):
            sem = self._sem("s_" + name)
            self.engs[name] = dict(e=e, sem=sem, cnt=0, seen={}, name=name)
        self.nbuf = 0
        self.ninst = 0
        self.bufs = []
        self.pstack = []
        self.semcnt = {}

    def _sem(self, name, persistent=False):
        g = self.nc.semaphore(name)
        s = g.__enter__()
        (self.pstack if persistent else self.stack).append(g)
        return s

    def sb(self, shape, dt, name=None):
        self.nbuf += 1
        name = f"{name or 'sb'}_{self.nbuf}"
        g = self.nc.sbuf_tensor(name, list(shape), dt)
        t = g.__enter__()
        self.stack.append(g)
        b = Buf(t, name)
        self.bufs.append(b)
        return b

    def ps(self, shape, dt=F32, name=None):
        self.nbuf += 1
        name = f"{name or 'ps'}_{self.nbuf}"
        g = self.nc.psum_tensor(name, list(shape), dt)
        t = g.__enter__()
        self.stack.append(g)
        b = Buf(t, name)
        self.bufs.append(b)
        return b

    def dram(self, name, shape, dt, kind="Internal"):
        t = self.nc.dram_tensor(name, list(shape), dt, kind=kind)
        b = Buf(t.ap(), name)
        b.persist = True
        self.bufs.append(b)
        return b

    def close(self):
        for g in reversed(self.stack):
            g.__exit__(None, None, None)
        self.stack = []
        for g in reversed(self.pstack):
            g.__exit__(None, None, None)
        self.pstack = []

    def _wait(self, E, sem, val):
        key = sem.num
        if E["seen"].get(key, 0) >= val:
            return
        E["seen"][key] = val
        E["e"].wait_ge(sem, val)

    def _split(self, reads, writes):
        rr = []; ww = []
        for b in writes:
            ww.append(b.root)
        for b in reads:
            b = b.root
            (ww if b.excl else rr).append(b)
        return rr, ww

    def _deps(self, E, reads, writes, skip_self=False):
        own = E["sem"].num
        for b in reads:
            for k, (s, v) in b.w.items():
                if k == own and (skip_self or not self.same_engine_sync):
                    continue
                self._wait(E, s, v)
        for b in writes:
            for k, (s, v) in list(b.w.items()) + list(b.r.items()):
                if k == own and (skip_self or not self.same_engine_sync):
                    continue
                self._wait(E, s, v)

    def _record(self, sem, val, reads, writes):
        k = sem.num
        for b in reads:
            b.r[k] = (sem, val)
        for b in writes:
            b.w = {k: (sem, val)}
            b.r = {}

    def op(self, eng, fn, reads=(), writes=(), inc=True, skip_self=False):
        E = self.engs[eng]
        reads, writes = self._split(reads, writes)
        self._deps(E, reads, writes, skip_self=skip_self)
        ins = fn(E["e"])
        self.ninst += 1
        if inc:
            E["cnt"] += 1
            ins.then_inc(E["sem"], 1)
            self._record(E["sem"], E["cnt"], reads, writes)
        else:
            self._record(E["sem"], E["cnt"] + 1, reads, writes)
        return ins

    def dma(self, q, out_ap, in_ap, dst: Buf, src: Buf, **kw):
        E = self.engs[q]
        dst = dst.root; src = src.root
        if dst.dsem is None:
            dst.dsem = self._sem("d_" + dst.name, persistent=dst.persist)
            dst.dcount = self.semcnt.get(dst.dsem.num, 0)
        dk = dst.dsem.num
        for k_, (s_, v_) in src.w.items():
            self._wait(E, s_, v_)
        for k_, (s_, v_) in list(dst.w.items()) + list(dst.r.items()):
            if k_ == dk:
                continue
            self._wait(E, s_, v_)
        ins = E["e"].dma_start(out=out_ap, in_=in_ap, **kw)
        dst.dcount += 16
        self.semcnt[dst.dsem.num] = dst.dcount
        ins.then_inc(dst.dsem, 16)
        self.ninst += 1
        k = dst.dsem.num
        src.r[k] = (dst.dsem, dst.dcount)
        dst.w = {k: (dst.dsem, dst.dcount)}
        dst.r = {}
        return ins

    def finish(self, bufs):
        E = self.engs["sp"]
        for b in bufs:
            for k, (s, v) in b.w.items():
                self._wait(E, s, v)

    def mark(self):
        return (len(self.stack), len(self.bufs))

    def release(self, mark):
        ns, nb = mark
        while len(self.stack) > ns:
            g = self.stack.pop()
            g.__exit__(None, None, None)
        del self.bufs[nb:]

    def barrier(self, dma_bufs=()):
        for en, E in self.engs.items():
            for fn, F in self.engs.items():
                if fn == en or F["cnt"] == 0:
                    continue
                self._wait(E, F["sem"], F["cnt"])
            for b in self.bufs:
                if b.dsem is not None:
                    self._wait(E, b.dsem, b.dcount)
        for b in self.bufs:
            b.w = {}
            b.r = {}

    def collective(self, kind, dst: Buf, src: Buf, groups):
        E = self.engs["pool"]
        for k_, (s_, v_) in src.w.items():
            self._wait(E, s_, v_)
        for k_, (s_, v_) in list(dst.w.items()) + list(dst.r.items()):
            self._wait(E, s_, v_)
        E["e"].collective_compute(kind, ALU.bypass, replica_groups=groups, ins=[src.t], outs=[dst.t])
        self.ninst += 1
        if not hasattr(self, "_ccd"):
            self._ccd = self.sb([128, 8], F32, "cc_dummy")
        d = self._ccd
        ins = E["e"].memset(d[:], 0.0)
        E["cnt"] += 1
        ins.then_inc(E["sem"], 1)
        k = id(E["sem"])
        src.r[k] = (E["sem"], E["cnt"])
        dst.w = {k: (E["sem"], E["cnt"])}
        dst.r = {}
        return ins


P = 128
EPS = 1e-6


def cdiv(a, b):
    return (a + b - 1) // b


def phaseA(kb, xT, gain_sb, hT, D, T):
    KC = D // P
    G4 = 4
    m = kb.mark()
    ones = kb.sb([P, P], F32, "A_ones")
    kb.op("pool", lambda e: e.memset(ones[:], 1.0), writes=[ones])
    epst = kb.sb([P, 1], F32, "A_eps")
    kb.op("pool", lambda e: e.memset(epst[:], EPS), writes=[epst])
    xs = [kb.sb([P, G4, 512], F32, f"A_x{i}") for i in range(2)]
    sq = [kb.sb([P, 512], F32, f"A_sq{i}") for i in range(2)]
    hs = [kb.sb([P, G4, 512], BF16, f"A_h{i}") for i in range(2)]
    ss = kb.ps([P, 512], F32, "A_ss")
    rstd = kb.sb([P, 512], F32, "A_rstd")
    xv = xT.t.rearrange("(c p) t -> p c t", p=P)
    hv = hT.t.rearrange("(c p) t -> p c t", p=P)
    n = 0
    for tg in range(T // 512):
        tsl = slice(tg * 512, (tg + 1) * 512)
        for c4 in range(KC // G4):
            xb = xs[n % 2]; n += 1
            kb.dma("sp", xb[:], xv[:, c4 * G4:(c4 + 1) * G4, tsl], xb, xT)
            for i in range(G4):
                kc = c4 * G4 + i
                s = sq[kc % 2]
                kb.op("act", lambda e: e.activation(s[:], xb[:, i, :], AF.Square), reads=[xb], writes=[s])
                kb.op("pe", lambda e: e.matmul(ss[:], ones[:], s[:], start=(kc == 0), stop=(kc == KC - 1)),
                      reads=[ones, s], writes=[ss], skip_self=True)
        kb.op("act", lambda e: e.activation(rstd[:], ss[:], AF.Sqrt, bias=epst[:], scale=1.0 / D),
              reads=[ss, epst], writes=[rstd])
        kb.op("dve", lambda e: e.reciprocal(rstd[:], rstd[:]), reads=[rstd], writes=[rstd])
        for c4 in range(KC // G4):
            xb = xs[n % 2]; hb = hs[n % 2]; n += 1
            kb.dma("sp", xb[:], xv[:, c4 * G4:(c4 + 1) * G4, tsl], xb, xT)
            for i in range(G4):
                kc = c4 * G4 + i
                kb.op("dve", lambda e: e.scalar_tensor_tensor(hb[:, i, :], xb[:, i, :], gain_sb[:, kc:kc + 1], rstd[:],
                                                              ALU.mult, ALU.mult),
                      reads=[xb, gain_sb, rstd], writes=[hb])
            kb.dma("sp", hv[:, c4 * G4:(c4 + 1) * G4, tsl], hb[:], hT, hb)
    kb.barrier([hT])
    kb.release(m)


def load_kc_tile(kb, q, dst, dst_ap_fn, src, src_view, KC, nsplit=4):
    step = cdiv(KC, nsplit)
    for c0 in range(0, KC, step):
        c1 = min(KC, c0 + step)
        kb.dma(q, dst_ap_fn(c0, c1), src_view[:, c0:c1, :], dst, src)


def phaseB1(kb, hT_parts, Tq, W, pf, pt, D, S, NF, NT, wcol0=0):
    KC = D // P
    m = kb.mark()
    wts = [kb.sb([P, KC, 512], BF16, f"B_w{i}") for i in range(2)]
    hts = [kb.sb([P, KC, 512], BF16, f"B_h{i}") for i in range(2)]
    osts = [kb.sb([P, 512], F32, f"B_o{i}") for i in range(4)]
    pss = [kb.ps([P, 512], F32, f"B_ps{i}") for i in range(4)]
    Wv = W.t.rearrange("(c p) n -> p c n", p=P)
    hviews = [h.t.rearrange("(c p) t -> p c t", p=P) for h in hT_parts]
    NTG = S // 512
    nfb = NF // 512
    ntb = NT // 512
    rem = NT - ntb * 512
    blocks = [("f", i * 512, 512) for i in range(nfb)] + [("t", NF + i * 512, 512) for i in range(ntb)]
    if rem:
        blocks.append(("t", NF + ntb * 512, rem))
    Wv = Wv[:, :, wcol0:wcol0 + NF + NT]
    nw = 0; k = 0
    iters = [(bi, tg) for bi in range(len(blocks)) for tg in range(NTG)]

    def load_h(i):
        _, tg_ = iters[i]
        ht_ = hts[i % 2]
        part = (tg_ * 512) // Tq
        off = tg_ * 512 - part * Tq
        load_kc_tile(kb, "sp", ht_, lambda a, b: ht_[:, a:b, :], hT_parts[part],
                     hviews[part][:, :, off:off + 512], KC)
    load_h(0)
    wt = None
    for it, (bi, tg) in enumerate(iters):
        kind, c0, cw = blocks[bi]
        if tg == 0:
            wt = wts[nw % 2]; nw += 1
            load_kc_tile(kb, "pool", wt, lambda a, b: wt[:, a:b, 0:cw], W, Wv[:, :, c0:c0 + cw], KC)
        if it + 1 < len(iters):
            load_h(it + 1)
        ht = hts[it % 2]
        if True:
            for sub in range(4):
                if kind == "f" and sub * 128 >= cw:
                    continue
                ps = pss[k % 4]; ost = osts[k % 4]
                for kc in range(KC):
                    if kind == "f":
                        kb.op("pe", lambda e: e.matmul(ps[:], wt[:, kc, sub * 128:(sub + 1) * 128], ht[:, kc, :],
                                                       start=(kc == 0), stop=(kc == KC - 1)),
                              reads=[wt, ht], writes=[ps], inc=(kc == KC - 1), skip_self=True)
                    else:
                        kb.op("pe", lambda e: e.matmul(ps[:, 0:cw], ht[:, kc, sub * 128:(sub + 1) * 128], wt[:, kc, 0:cw],
                                                       start=(kc == 0), stop=(kc == KC - 1)),
                              reads=[wt, ht], writes=[ps], inc=(kc == KC - 1), skip_self=True)
                ww = 512 if kind == "f" else cw
                if k % 2 == 0:
                    kb.op("act", lambda e: e.copy(ost[:, 0:ww], ps[:, 0:ww]), reads=[ps], writes=[ost])
                else:
                    kb.op("dve", lambda e: e.tensor_copy(ost[:, 0:ww], ps[:, 0:ww]), reads=[ps], writes=[ost])
                if kind == "f":
                    kb.dma("sp", pf[c0 + sub * 128:c0 + (sub + 1) * 128, tg * 512:(tg + 1) * 512], ost[:], pf, ost)
                else:
                    r0 = tg * 512 + sub * 128
                    kb.dma("sp", pt[r0:r0 + 128, c0 - NF:c0 - NF + cw], ost[:, 0:cw], pt, ost)
                k += 1
    kb.barrier([pf, pt])
    kb.release(m)


def phaseC(kb, hT_own, yT_all, xT_own, Wg, Wb, Wo, xoutT, D, T, BW=2048, scratch=None):
    KC = D // P
    KBW = BW // P
    m = kb.mark()
    ht = kb.sb([P, KC, 512], BF16, "C_h")
    yt = kb.sb([P, 3 * KBW, 512], BF16, "C_y")
    mg = kb.sb([P, KC, 512], BF16, "C_m")
    wgs = [kb.sb([P, KC, P], BF16, f"C_wg{i}") for i in range(2)]
    wbs = [kb.sb([P, KBW, P], BF16, f"C_wb{i}") for i in range(2)]
    wos = [kb.sb([P, KC, P], BF16, f"C_wo{i}") for i in range(2)]
    sgs = [kb.sb([P, 512], F32, f"C_sg{i}") for i in range(2)]
    acc = kb.sb([P, 512], F32, "C_acc")
    tmp = kb.sb([P, 512], F32, "C_tmp")
    xs = [kb.sb([P, 512], F32, f"C_x{i}") for i in range(2)]
    os_ = [kb.sb([P, 512], F32, f"C_o{i}") for i in range(2)]
    psg = [kb.ps([P, 512], F32, f"C_psg{i}") for i in range(2)]
    psb = [kb.ps([P, 512], F32, f"C_psb{i}") for i in range(2)]
    pso = [kb.ps([P, 512], F32, f"C_pso{i}") for i in range(2)]
    hv = hT_own.t.rearrange("(c p) t -> p c t", p=P)
    yv = yT_all.t.rearrange("(c p) t -> p c t", p=P)
    Wgv = Wg.t.rearrange("(c p) n -> p c n", p=P)
    Wbv = Wb.t.rearrange("(c p) n -> p c n", p=P)
    Wov = Wo.t.rearrange("(c p) n -> p c n", p=P)
    u = 0; v = 0
    if scratch is not None:
        Wg16, Wb16, Wo16 = scratch
        pu = 0
        for dc in range(KC):
            for n in range(3):
                wg = wgs[pu % 2]; wb = wbs[pu % 2]; pu += 1
                c0 = n * D + dc * P
                load_kc_tile(kb, "pool", wg, lambda a, b: wg[:, a:b, :], Wg, Wgv[:, :, c0:c0 + P], KC, nsplit=2)
                load_kc_tile(kb, "pool", wb, lambda a, b: wb[:, a:b, :], Wb,
                             Wbv[:, n * KBW:(n + 1) * KBW, dc * P:(dc + 1) * P], KBW, nsplit=1)
                kb.dma("sp", Wg16[dc * 3 + n].rearrange("p (k c) -> p k c", c=P), wg[:], Wg16, wg)
                kb.dma("sp", Wb16[dc * 3 + n].rearrange("p (k c) -> p k c", c=P), wb[:], Wb16, wb)
        for ec in range(KC):
            wo = wos[pu % 2]; pu += 1
            load_kc_tile(kb, "pool", wo, lambda a, b: wo[:, a:b, :], Wo, Wov[:, :, ec * P:(ec + 1) * P], KC, nsplit=2)
            kb.dma("sp", Wo16[ec].rearrange("p (k c) -> p k c", c=P), wo[:], Wo16, wo)
    for tg in range(T // 512):
        tsl = slice(tg * 512, (tg + 1) * 512)
        load_kc_tile(kb, "sp", ht, lambda a, b: ht[:, a:b, :], hT_own, hv[:, :, tsl], KC)
        load_kc_tile(kb, "sp", yt, lambda a, b: yt[:, a:b, :], yT_all, yv[:, :, tsl], 3 * KBW)
        for dc in range(KC):
            for n in range(3):
                wg = wgs[u % 2]; wb = wbs[u % 2]; pg = psg[u % 2]; pb = psb[u % 2]; sg = sgs[u % 2]; u += 1
                c0 = n * D + dc * P
                if scratch is not None:
                    kb.dma("sp", wg[:], Wg16[dc * 3 + n].rearrange("p (k c) -> p k c", c=P), wg, Wg16)
                    kb.dma("sp", wb[:], Wb16[dc * 3 + n].rearrange("p (k c) -> p k c", c=P), wb, Wb16)
                else:
                    load_kc_tile(kb, "pool", wg, lambda a, b: wg[:, a:b, :], Wg, Wgv[:, :, c0:c0 + P], KC, nsplit=2)
                    load_kc_tile(kb, "pool", wb, lambda a, b: wb[:, a:b, :], Wb,
                                 Wbv[:, n * KBW:(n + 1) * KBW, dc * P:(dc + 1) * P], KBW, nsplit=1)
                for kc in range(KC):
                    kb.op("pe", lambda e: e.matmul(pg[:], wg[:, kc, :], ht[:, kc, :], start=(kc == 0), stop=(kc == KC - 1)),
                          reads=[wg, ht], writes=[pg], inc=(kc == KC - 1), skip_self=True)
                for kc in range(KBW):
                    kb.op("pe", lambda e: e.matmul(pb[:], wb[:, kc, :], yt[:, n * KBW + kc, :], start=(kc == 0), stop=(kc == KBW - 1)),
                          reads=[wb, yt], writes=[pb], inc=(kc == KBW - 1), skip_self=True)
                kb.op("act", lambda e: e.activation(sg[:], pg[:], AF.Sigmoid), reads=[pg], writes=[sg])
                if n == 0:
                    kb.op("dve", lambda e: e.tensor_tensor(acc[:], pb[:], sg[:], ALU.mult), reads=[pb, sg], writes=[acc])
                elif n == 1:
                    kb.op("dve", lambda e: e.tensor_tensor(tmp[:], pb[:], sg[:], ALU.mult), reads=[pb, sg], writes=[tmp])
                    kb.op("pool", lambda e: e.tensor_tensor(acc[:], acc[:], tmp[:], ALU.add), reads=[acc, tmp], writes=[acc])
                else:
                    kb.op("dve", lambda e: e.tensor_tensor(tmp[:], pb[:], sg[:], ALU.mult), reads=[pb, sg], writes=[tmp])
                    kb.op("pool", lambda e: e.tensor_tensor(mg[:, dc, :], acc[:], tmp[:], ALU.add), reads=[acc, tmp], writes=[mg])
        def load_o(ec_, vv):
            wo_ = wos[vv % 2]; xb_ = xs[vv % 2]
            if scratch is not None:
                kb.dma("sp", wo_[:], Wo16[ec_].rearrange("p (k c) -> p k c", c=P), wo_, Wo16)
            else:
                load_kc_tile(kb, "pool", wo_, lambda a, b: wo_[:, a:b, :], Wo, Wov[:, :, ec_ * P:(ec_ + 1) * P], KC, nsplit=2)
            kb.dma("sp", xb_[:], xT_own[ec_ * P:(ec_ + 1) * P, tsl], xb_, xT_own)
        load_o(0, v)
        for ec in range(KC):
            wo = wos[v % 2]; po = pso[v % 2]; xb = xs[v % 2]; ob = os_[v % 2]
            if ec + 1 < KC:
                load_o(ec + 1, v + 1)
            v += 1
            for kc in range(KC):
                kb.op("pe", lambda e: e.matmul(po[:], wo[:, kc, :], mg[:, kc, :], start=(kc == 0), stop=(kc == KC - 1)),
                      reads=[wo, mg], writes=[po], inc=(kc == KC - 1), skip_self=True)
            kb.op("dve", lambda e: e.tensor_tensor(ob[:], po[:], xb[:], ALU.add), reads=[po, xb], writes=[ob])
            kb.dma("sp", xoutT[ec * P:(ec + 1) * P, tsl], ob[:], xoutT, ob)
    kb.barrier([xoutT])
    kb.release(m)


P = 128
EPS = 1e-6
RET_Q0 = 0
RET_K0 = 512
RET_V0 = 0
RET_Z0 = 512


def y_prep(kb, C, slot, zsrc_ap, pt, gain_ap, gain_buf, W):
    zt = C["zt"][slot]; gz = C["gz"][slot]
    kb.dma("sp", zt[:, 0:W], zsrc_ap, zt, pt)
    kb.op("act", lambda e: e.activation(gz[:, 0:W], zt[:, 0:W], AF.Silu), reads=[zt], writes=[gz])
    kb.op("pool", lambda e: e.tensor_tensor(gz[:, 0:W], gz[:, 0:W], gain_ap, ALU.mult), reads=[gz, gain_buf], writes=[gz])


def y_tail(kb, C, on, slot, W, ytile_store):
    gz = C["gz"][slot]; yb = C["yb"][C["n"] % 2]
    C["n"] += 1
    kb.op("pool", lambda e: e.tensor_tensor(yb[:, 0:W], on[:, 0:W], gz[:, 0:W], ALU.mult), reads=[on, gz], writes=[yb])
    for c in range(W // P):
        pT = C["pT"][C["nt"] % 2]; C["nt"] += 1
        kb.op("pe", lambda e: e.transpose(pT[:], yb[:, c * P:(c + 1) * P], C["identb"][:]),
              reads=[yb, C["identb"]], writes=[pT])
        ytile_store(c, pT)


def tail_ctx(kb, pfx, W):
    C = dict(n=0, nt=0)
    C["zt"] = [kb.sb([P, W], F32, f"{pfx}_zt{i}") for i in range(8)]
    C["gz"] = [kb.sb([P, W], F32, f"{pfx}_gz{i}") for i in range(8)]
    C["yb"] = [kb.sb([P, W], BF16, f"{pfx}_yb{i}") for i in range(2)]
    C["pT"] = [kb.ps([P, P], BF16, f"{pfx}_pT{i}") for i in range(2)]
    identf = kb.sb([P, P], F32, f"{pfx}_identf")
    kb.op("pool", lambda e: e.memset(identf[:], 1.0), writes=[identf])
    kb.op("pool", lambda e: e.affine_select(out=identf[:], in_=identf[:], pattern=[[-1, P]], compare_op=ALU.is_equal,
                                            fill=0.0, base=0, channel_multiplier=1), reads=[identf], writes=[identf])
    identb = kb.sb([P, P], BF16, f"{pfx}_identb")
    kb.op("pool", lambda e: e.tensor_copy(identb[:], identf[:]), reads=[identf], writes=[identb])
    C["identb"] = identb
    C["identf"] = identf
    return C


def ret_mixer(kb, pf, pt, cs_d, m0_d, md_d, cf_d, gnb_d, yT, S, ybase=0):
    NB = S // P
    NG = S // 512
    m = kb.mark()
    cos = kb.sb([P, S], F32, "R_cos"); sin = kb.sb([P, S], F32, "R_sin")
    kb.dma("sp", cos[:], cs_d[0], cos, cs_d); kb.dma("sp", sin[:], cs_d[1], sin, cs_d)
    gnb = kb.sb([P, 512], F32, "R_gnb"); kb.dma("sp", gnb[:], gnb_d[:], gnb, gnb_d)
    m0 = kb.sb([P, 512], F32, "R_m0"); md = kb.sb([P, 4, 512], F32, "R_md"); cf = kb.sb([P, NB], F32, "R_cf")
    qr = kb.sb([P, 2, S], BF16, "R_qr"); kr = kb.sb([P, 2, S], BF16, "R_kr")
    v16 = kb.sb([P, NB, 256], BF16, "R_v16")
    ld = [kb.sb([P, 2, 512], F32, f"R_ld{i}") for i in range(2)]
    tms = [kb.sb([P, 512], F32, f"R_tm{i}") for i in range(4)]
    pTs = [kb.sb([P, 512], BF16, f"R_p{i}") for i in range(3)]
    sT = [kb.ps([P, 512], F32, f"R_sT{i}") for i in range(2)]
    oacc = [kb.ps([P, 256], F32, f"R_o{i}") for i in range(4)]
    st = kb.sb([P, 6], F32, "R_st"); mv = kb.sb([P, 2], F32, "R_mv"); rs = kb.sb([P, 1], F32, "R_rs")
    epst = kb.sb([P, 1], F32, "R_eps")
    kb.op("pool", lambda e: e.memset(epst[:], EPS), writes=[epst])
    ons = [kb.sb([P, 256], F32, f"R_on{i}") for i in range(2)]
    yTs = [kb.sb([P, 2, 512], BF16, f"R_yT{i}") for i in range(2)]
    C = tail_ctx(kb, "R", 256)
    nl = 0; npp = 0; ns = 0; non = 0; ny = 0
    for hh in range(2):
        kb.dma("sp", m0[:], m0_d[hh], m0, m0_d)
        kb.dma("sp", md[:], md_d[hh].rearrange("r p q -> p r q"), md, md_d)
        kb.dma("sp", cf[:], cf_d[hh], cf, cf_d)
        kb.dma("pool", v16[:], pt[:, RET_V0 + hh * 256:RET_V0 + (hh + 1) * 256].rearrange("(t p) c -> p t c", p=P), v16, pt)
        for (dst, row0) in ((qr, RET_Q0 + hh * 256), (kr, RET_K0 + hh * 256)):
            for tg in range(NG):
                tsl = slice(tg * 512, (tg + 1) * 512)
                xb = ld[nl % 2]; nl += 1
                kb.dma("sp", xb[:], pf[row0:row0 + 256, tsl].rearrange("(h p) t -> p h t", p=P), xb, pf)
                t1, t2, t3, t4 = tms
                kb.op("dve", lambda e: e.tensor_tensor(t1[:], xb[:, 0, :], cos[:, tsl], ALU.mult), reads=[xb, cos], writes=[t1])
                kb.op("pool", lambda e: e.tensor_tensor(t2[:], xb[:, 1, :], sin[:, tsl], ALU.mult), reads=[xb, sin], writes=[t2])
                kb.op("dve", lambda e: e.tensor_tensor(dst[:, 0, tsl], t1[:], t2[:], ALU.subtract), reads=[t1, t2], writes=[dst])
                kb.op("pool", lambda e: e.tensor_tensor(t3[:], xb[:, 0, :], sin[:, tsl], ALU.mult), reads=[xb, sin], writes=[t3])
                kb.op("dve", lambda e: e.tensor_tensor(t4[:], xb[:, 1, :], cos[:, tsl], ALU.mult), reads=[xb, cos], writes=[t4])
                kb.op("pool", lambda e: e.tensor_tensor(dst[:, 1, tsl], t3[:], t4[:], ALU.add), reads=[t3, t4], writes=[dst])
        items = [(G, j) for G in range(NG) for j in range(4 * G + 4)]
        pend = {}

        def emit_s(it):
            nonlocal ns
            G, j = it
            ps = sT[ns % 2]; ns += 1
            ksl = slice(j * P, (j + 1) * P); qsl_ = slice(G * 512, (G + 1) * 512)
            for hf in range(2):
                kb.op("pe", lambda e: e.matmul(ps[:], kr[:, hf, ksl], qr[:, hf, qsl_], start=(hf == 0), stop=(hf == 1)),
                      reads=[kr, qr], writes=[ps], inc=(hf == 1), skip_self=True)
            pend[it] = ps

        def emit_rest(it):
            nonlocal npp
            G, j = it
            ps = pend.pop(it)
            pb = pTs[npp % 3]; npp += 1
            r = j - 4 * G
            if r < 0:
                d = 4 * G - j - 1
                kb.op("dve", lambda e: e.scalar_tensor_tensor(pb[:], ps[:], cf[:, d:d + 1], m0[:], ALU.mult, ALU.mult),
                      reads=[ps, cf, m0], writes=[pb])
            else:
                kb.op("dve", lambda e: e.tensor_tensor(pb[:], ps[:], md[:, r, :], ALU.mult), reads=[ps, md], writes=[pb])
            for qs in range(4):
                if r > qs:
                    continue
                first = (j == 0)
                last = (j == 4 * G + qs)
                kb.op("pe", lambda e: e.matmul(oacc[qs][:], pb[:, qs * P:(qs + 1) * P], v16[:, j, :], start=first, stop=last),
                      reads=[pb, v16], writes=[oacc[qs]], skip_self=True)
        emit_s(items[0])
        for ii, it in enumerate(items):
            if ii + 1 < len(items):
                emit_s(items[ii + 1])
            G, j = it
            if j == 0:
                for qs_ in range(4):
                    tt_ = 4 * G + qs_
                    y_prep(kb, C, (G % 2) * 4 + qs_, pt[tt_ * P:(tt_ + 1) * P, RET_Z0 + hh * 256:RET_Z0 + (hh + 1) * 256], pt,
                           gnb[:, hh * 256:(hh + 1) * 256], gnb, 256)
            emit_rest(it)
            if j != 4 * G + 3:
                continue
            qsl = slice(G * 512, (G + 1) * 512)
            yts = yTs[ny % 2]; ny += 1
            for qs in range(4):
                tt = 4 * G + qs
                on = ons[non % 2]; non += 1
                o = oacc[qs]
                kb.op("dve", lambda e: e.bn_stats(st[:], o[:]), reads=[o], writes=[st])
                kb.op("dve", lambda e: e.bn_aggr(mv[:], st[:]), reads=[st], writes=[mv])
                kb.op("act", lambda e: e.activation(rs[:], mv[:, 1:2], AF.Sqrt, bias=epst[:]), reads=[mv, epst], writes=[rs])
                kb.op("dve", lambda e: e.reciprocal(rs[:], rs[:]), reads=[rs], writes=[rs])
                kb.op("dve", lambda e: e.tensor_scalar(on[:], o[:], mv[:, 0:1], rs[:], ALU.subtract, ALU.mult),
                      reads=[o, mv, rs], writes=[on])

                def store(c, pT, qs=qs, yts=yts):
                    kb.op("act", lambda e: e.copy(yts[:, c, qs * P:(qs + 1) * P], pT[:]), reads=[pT], writes=[yts])
                y_tail(kb, C, on, (G % 2) * 4 + qs, 256, store)
            for c in range(2):
                r0 = ybase + hh * 256 + c * P
                kb.dma("sp", yT[r0:r0 + P, qsl], yts[:, c, :], yT, yts)
    kb.barrier([yT])
    kb.release(m)


def ret_consts(S, heads):
    import numpy as np
    half = 128
    inv = (10000.0 ** (-np.arange(half, dtype=np.float32) / half)).astype(np.float32)
    ang = np.arange(S, dtype=np.float32)[None, :] * inv[:, None]
    cs = np.stack([np.cos(ang), np.sin(ang)]).astype(np.float32)
    NB = S // 128
    m0 = np.zeros((2, 128, 512), np.float32); md = np.zeros((2, 4, 128, 512), np.float32)
    cf = np.zeros((2, 128, NB), np.float32)
    ki = np.arange(128, dtype=np.float64)[:, None]; qi = np.arange(512, dtype=np.float64)[None, :]
    for a, h in enumerate(heads):
        lg = np.log(1.0 - 2.0 ** (-5.0 - h))
        m0[a] = np.exp(lg * (qi - ki + 128)) / 16.0
        for r in range(4):
            rel = qi - 128 * r - ki
            md[a, r] = np.where(rel >= 0, np.exp(lg * np.maximum(rel, 0)) / 16.0, 0.0)
        cf[a] = np.exp(lg * 128.0 * np.arange(NB))[None, :]
    return cs, m0, md, cf

import math


P = 128
EPS = 1e-6
DF_Q0 = 1024
DF_K0 = 1536
DF_V0 = 1024
DF_Z0 = 1536
NEG = -30000.0


def diff_mixer(kb, pf, pt, bn_d, fb_d, gqk_d, lam_d, subb_d, yT, S, layer_idx, ybase=1024):
    NB = S // P
    NG = S // 512
    lam_init = 0.8 - 0.6 * math.exp(-0.3 * layer_idx)
    m = kb.mark()
    ones = kb.sb([P, P], F32, "D_ones")
    kb.op("pool", lambda e: e.memset(ones[:], 1.0), writes=[ones])
    fb = kb.sb([P, 2], F32, "D_fb"); kb.dma("sp", fb[:], fb_d[:], fb, fb_d)
    gqk = kb.sb([P, 2], F32, "D_gqk"); kb.dma("sp", gqk[:], gqk_d[:], gqk, gqk_d)
    subb = kb.sb([P, 256], F32, "D_subb"); kb.dma("sp", subb[:], subb_d[:], subb, subb_d)
    kb.op("pool", lambda e: e.tensor_scalar(subb[:], subb[:], 1.0 - lam_init, None, ALU.mult), reads=[subb], writes=[subb])
    lv = kb.sb([P, 4, P], F32, "D_lv"); kb.dma("sp", lv[:], lam_d.t.rearrange("a p d -> p a d"), lv, lam_d)
    lt = kb.sb([P, 2, P], F32, "D_lt"); ls = kb.sb([P, 2], F32, "D_ls"); lam = kb.sb([P, 1], F32, "D_lam")
    kb.op("dve", lambda e: e.tensor_tensor(lt[:, 0, :], lv[:, 0, :], lv[:, 1, :], ALU.mult), reads=[lv], writes=[lt])
    kb.op("dve", lambda e: e.tensor_tensor(lt[:, 1, :], lv[:, 2, :], lv[:, 3, :], ALU.mult), reads=[lv, lt], writes=[lt])
    kb.op("dve", lambda e: e.reduce_sum(ls[:], lt[:], AX.X), reads=[lt], writes=[ls])
    kb.op("act", lambda e: e.activation(ls[:], ls[:], AF.Exp), reads=[ls], writes=[ls])
    kb.op("dve", lambda e: e.tensor_tensor(lam[:], ls[:, 0:1], ls[:, 1:2], ALU.subtract), reads=[ls], writes=[lam])
    kb.op("dve", lambda e: e.tensor_scalar(lam[:], lam[:], lam_init, None, ALU.add), reads=[lam], writes=[lam])
    epsq = kb.sb([P, 3], F32, "D_eps")
    kb.op("pool", lambda e: e.memset(epsq[:, 0:1], 128.0 * EPS), writes=[epsq])
    kb.op("pool", lambda e: e.memset(epsq[:, 1:2], EPS), reads=[epsq], writes=[epsq])
    bn = kb.sb([P, 5, 512], F32, "D_bn")
    qn = kb.sb([P, 2, S], BF16, "D_qn"); kn = kb.sb([P, 2, S], BF16, "D_kn")
    v16 = kb.sb([P, NB, 257], BF16, "D_v16")
    kb.op("pool", lambda e: e.memset(v16[:, :, 256:257], 1.0), writes=[v16])
    ld = [kb.sb([P, 512], F32, f"D_ld{i}") for i in range(2)]
    sq = [kb.sb([P, 512], F32, f"D_sq{i}") for i in range(2)]
    rst = [kb.sb([P, 512], F32, f"D_rst{i}") for i in range(2)]
    tb = [kb.sb([P, 512], F32, f"D_tb{i}") for i in range(2)]
    pTs = [kb.sb([P, 512], BF16, f"D_p{i}") for i in range(3)]
    sT = [kb.ps([P, 512], F32, f"D_sT{i}") for i in range(2)]
    oacc = [kb.ps([P, 257], F32, f"D_o{i}") for i in range(4)]
    om0 = kb.sb([P, 4, 257], F32, "D_om0")
    rr = kb.sb([P, 4], F32, "D_rr")
    t1 = [kb.sb([P, 256], F32, f"D_t1{i}") for i in range(2)]
    ob = [kb.sb([P, 256], F32, f"D_ob{i}") for i in range(2)]
    junk = kb.sb([P, 256], F32, "D_junk")
    ons = [kb.sb([P, 256], F32, f"D_on{i}") for i in range(2)]
    yTs = [kb.sb([P, 2, 512], BF16, f"D_yT{i}") for i in range(2)]
    C = tail_ctx(kb, "D", 256)
    nl = 0; ns = 0; npp = 0; ntb = 0; nt1 = 0; non = 0; ny = 0
    for hh in range(2):
        kb.dma("sp", bn[:], bn_d[hh].rearrange("r p q -> p r q"), bn, bn_d)
        kb.dma("pool", v16[:, :, 0:256], pt[:, DF_V0 + hh * 256:DF_V0 + (hh + 1) * 256].rearrange("(t p) c -> p t c", p=P), v16, pt)
        for (dst, row0, gi) in ((qn, DF_Q0 + hh * 256, 0), (kn, DF_K0 + hh * 256, 1)):
            for mm in range(2):
                for tg in range(NG):
                    tsl = slice(tg * 512, (tg + 1) * 512)
                    xb = ld[nl % 2]; s2 = sq[nl % 2]; rs = rst[nl % 2]; nl += 1
                    ps = sT[ns % 2]; ns += 1
                    kb.dma("sp", xb[:], pf[row0 + mm * P:row0 + (mm + 1) * P, tsl], xb, pf)
                    kb.op("act", lambda e: e.activation(s2[:], xb[:], AF.Square), reads=[xb], writes=[s2])
                    kb.op("pe", lambda e: e.matmul(ps[:], ones[:], s2[:], start=True, stop=True), reads=[ones, s2], writes=[ps], skip_self=True)
                    sc = 1.0 if gi == 0 else 1.0 / 128.0
                    kb.op("act", lambda e: e.activation(rs[:], ps[:], AF.Sqrt, bias=epsq[:, gi:gi + 1], scale=sc),
                          reads=[ps, epsq], writes=[rs])
                    kb.op("dve", lambda e: e.reciprocal(rs[:], rs[:]), reads=[rs], writes=[rs])
                    kb.op("dve", lambda e: e.scalar_tensor_tensor(dst[:, mm, tsl], xb[:], gqk[:, gi:gi + 1], rs[:], ALU.mult, ALU.mult),
                          reads=[xb, gqk, rs], writes=[dst])
        items = [(G, mm, j) for G in range(NG) for mm in range(2) for j in range(4 * G + 4)]
        pend = {}

        def emit_s(it):
            nonlocal ns
            G, mm, j = it
            ps = sT[ns % 2]; ns += 1
            ksl = slice(j * P, (j + 1) * P); qsl_ = slice(G * 512, (G + 1) * 512)
            kb.op("pe", lambda e: e.matmul(ps[:], kn[:, mm, ksl], qn[:, mm, qsl_], start=True, stop=True),
                  reads=[kn, qn], writes=[ps], skip_self=True)
            pend[it] = ps

        def emit_rest(it):
            nonlocal npp, ntb
            G, mm, j = it
            ps = pend.pop(it)
            pb = pTs[npp % 3]; npp += 1
            r = j - 4 * G
            if r <= -2:
                kb.op("act", lambda e: e.activation(pb[:], ps[:], AF.Exp, bias=fb[:, hh:hh + 1]), reads=[ps, fb], writes=[pb])
            else:
                t = tb[ntb % 2]; ntb += 1
                kb.op("dve", lambda e: e.tensor_tensor(t[:], ps[:], bn[:, r + 1, :], ALU.add), reads=[ps, bn], writes=[t])
                kb.op("act", lambda e: e.activation(pb[:], t[:], AF.Exp), reads=[t], writes=[pb])
            for qs in range(4):
                if r > qs:
                    continue
                first = (j == 0)
                last = (j == 4 * G + qs)
                kb.op("pe", lambda e: e.matmul(oacc[qs][:], pb[:, qs * P:(qs + 1) * P], v16[:, j, :], start=first, stop=last),
                      reads=[pb, v16], writes=[oacc[qs]], skip_self=True)
            if mm == 0 and j == 4 * G + 3:
                for qs in range(4):
                    if qs % 2 == 0:
                        kb.op("act", lambda e: e.copy(om0[:, qs, :], oacc[qs][:]), reads=[oacc[qs]], writes=[om0])
                    else:
                        kb.op("dve", lambda e: e.tensor_copy(om0[:, qs, :], oacc[qs][:]), reads=[oacc[qs]], writes=[om0])
        emit_s(items[0])
        for ii, it in enumerate(items):
            if ii + 1 < len(items):
                emit_s(items[ii + 1])
            G, mm, j = it
            if mm == 0 and j == 0:
                for qs_ in range(4):
                    tt_ = 4 * G + qs_
                    y_prep(kb, C, (G % 2) * 4 + qs_, pt[tt_ * P:(tt_ + 1) * P, DF_Z0 + hh * 256:DF_Z0 + (hh + 1) * 256], pt,
                           subb[:], subb, 256)
            emit_rest(it)
            if not (mm == 1 and j == 4 * G + 3):
                continue
            qsl = slice(G * 512, (G + 1) * 512)
            yts = yTs[ny % 2]; ny += 1
            for qs in range(4):
                tt = 4 * G + qs
                o1 = oacc[qs]
                tq = t1[nt1 % 2]; o = ob[nt1 % 2]; nt1 += 1
                on = ons[non % 2]; non += 1
                kb.op("dve", lambda e: e.reciprocal(rr[:, 0:1], om0[:, qs, 256:257]), reads=[om0], writes=[rr])
                kb.op("dve", lambda e: e.reciprocal(rr[:, 1:2], o1[:, 256:257]), reads=[o1, rr], writes=[rr])
                kb.op("dve", lambda e: e.tensor_tensor(rr[:, 1:2], rr[:, 1:2], lam[:], ALU.mult), reads=[rr, lam], writes=[rr])
                kb.op("dve", lambda e: e.tensor_scalar(tq[:], o1[:, 0:256], rr[:, 1:2], None, ALU.mult), reads=[o1, rr], writes=[tq])
                kb.op("dve", lambda e: e.scalar_tensor_tensor(o[:], om0[:, qs, 0:256], rr[:, 0:1], tq[:], ALU.mult, ALU.subtract),
                      reads=[om0, rr, tq], writes=[o])
                kb.op("act", lambda e: e.activation(junk[:], o[:], AF.Square, accum_out=rr[:, 2:3]), reads=[o, rr], writes=[junk, rr])
                kb.op("act", lambda e: e.activation(rr[:, 3:4], rr[:, 2:3], AF.Sqrt, bias=epsq[:, 1:2], scale=1.0 / 256.0),
                      reads=[rr, epsq], writes=[rr])
                kb.op("dve", lambda e: e.reciprocal(rr[:, 3:4], rr[:, 3:4]), reads=[rr], writes=[rr])
                kb.op("dve", lambda e: e.tensor_scalar(on[:], o[:], rr[:, 3:4], None, ALU.mult), reads=[o, rr], writes=[on])

                def store(c, pT, qs=qs, yts=yts):
                    kb.op("act", lambda e: e.copy(yts[:, c, qs * P:(qs + 1) * P], pT[:]), reads=[pT], writes=[yts])
                y_tail(kb, C, on, (G % 2) * 4 + qs, 256, store)
            for c in range(2):
                r0 = ybase + hh * 256 + c * P
                kb.dma("sp", yT[r0:r0 + P, qsl], yts[:, c, :], yT, yts)
    kb.barrier([yT])
    kb.release(m)


def rel_bucket_np(rel):
    import numpy as np
    import jax
    import jax.numpy as jnp
    with jax.default_device(jax.devices("cpu")[0]):
        return _rel_bucket_cpu(np.asarray(rel))


def _rel_bucket_cpu(rel):
    import numpy as np
    import jax.numpy as jnp
    rel = jnp.asarray(rel)
    nb = 32 // 2
    max_exact = nb // 2
    base = jnp.where(rel > 0, nb, 0)
    n = jnp.abs(rel)
    nf = jnp.maximum(n, 1).astype(jnp.float32)
    large = max_exact + (jnp.log(nf / max_exact) / math.log(128 / max_exact) * (nb - max_exact)).astype(jnp.int32)
    large = jnp.minimum(large, nb - 1)
    return np.asarray(base + jnp.where(n < max_exact, n, large))


def diff_bias_index():
    import numpy as np
    ki = np.arange(128)[:, None]; qi = np.arange(512)[None, :]
    idx = np.zeros((5, 128, 512), np.int64); vis = np.zeros((5, 128, 512), bool)
    for p in range(5):
        krel = 128 * (p - 1) + ki
        rel = krel - qi
        idx[p] = rel_bucket_np(rel)
        vis[p] = (krel // 64) <= (qi // 64)
    return idx, vis


P = 128
EPS = 1e-6
GD_Q0 = 2048
GD_K0 = 2560
GD_V0 = 3072
GD_Z0 = 2048
GD_AB0 = 2560
NEGM = -30000.0
NLEV = 6


class PsumCarver:
    def __init__(self, kb, pfx, nbanks_f32, nbanks_bf16=1):
        self.f = [kb.ps([P, 512], F32, f"{pfx}_bk{i}") for i in range(nbanks_f32)]
        self.b = [kb.ps([P, 1024], BF16, f"{pfx}_bb{i}") for i in range(nbanks_bf16)]
        for x in self.f + self.b:
            x.excl = True

    def f32(self, bank, slot, name):
        return self.f[bank].view(self.f[bank][:, slot * P:(slot + 1) * P], name)

    def bf(self, bank, slot, name):
        return self.b[bank].view(self.b[bank][:, slot * P:(slot + 1) * P], name)


def gdn_mixer(kb, pf, pt, cw_d, dtb_d, alog_d, gnb_d, gm_d, yT, S, ybase=512):
    NB = S // P
    NG = S // 512
    NC4 = NB * 4
    m = kb.mark()
    cw = kb.sb([P, 48], F32, "G_cw"); kb.dma("sp", cw[:], cw_d[:], cw, cw_d)
    gnb = kb.sb([P, P], F32, "G_gnb"); kb.dma("sp", gnb[:], gnb_d[:], gnb, gnb_d)
    gm = kb.sb([P, 4, P], F32, "G_gm"); kb.dma("sp", gm[:], gm_d.t.rearrange("a p q -> p a q"), gm, gm_d)
    UT = gm[:, 0, :]; SEL = gm[:, 1, :]; NEGL = gm[:, 2, :]; NEGU = gm[:, 3, :]
    ones = kb.sb([P, P], F32, "G_ones"); kb.op("pool", lambda e: e.memset(ones[:], 1.0), writes=[ones])
    nones = kb.sb([P, P], F32, "G_nones"); kb.op("pool", lambda e: e.memset(nones[:], -1.0), writes=[nones])
    identf = kb.sb([P, P], F32, "G_identf")
    kb.op("pool", lambda e: e.memset(identf[:], 1.0), writes=[identf])
    kb.op("pool", lambda e: e.affine_select(out=identf[:], in_=identf[:], pattern=[[-1, P]], compare_op=ALU.is_equal,
                                            fill=0.0, base=0, channel_multiplier=1), reads=[identf], writes=[identf])
    identb = kb.sb([P, P], BF16, "G_identb")
    kb.op("pool", lambda e: e.tensor_copy(identb[:], identf[:]), reads=[identf], writes=[identb])
    epst = kb.sb([P, 2], F32, "G_eps")
    kb.op("pool", lambda e: e.memset(epst[:, 0:1], EPS), writes=[epst])
    kb.op("pool", lambda e: e.memset(epst[:, 1:2], 128.0 * EPS), reads=[epst], writes=[epst])
    PS = PsumCarver(kb, "G", 7, 1)
    ab = kb.sb([P, NB, 8], F32, "G_ab")
    kb.dma("sp", ab[:], pt[:, GD_AB0:GD_AB0 + 8].rearrange("(t p) c -> p t c", p=P), ab, pt)
    dtb = kb.sb([P, NB, 4], F32, "G_dtb"); kb.dma("sp", dtb[:], dtb_d.t.rearrange("p (t h) -> p t h", h=4), dtb, dtb_d)
    alog = kb.sb([P, NB, 4], F32, "G_alog"); kb.dma("sp", alog[:], alog_d.t.rearrange("p (t h) -> p t h", h=4), alog, alog_d)
    beta = kb.sb([P, NB, 4], F32, "G_beta"); g = kb.sb([P, NB, 4], F32, "G_g"); sp_ = kb.sb([P, NB, 4], F32, "G_sp")
    gc = kb.sb([P, NB, 4], F32, "G_gc"); eg = kb.sb([P, NB, 4], F32, "G_eg"); bg = kb.sb([P, NB, 4], F32, "G_bg")
    kgs = kb.sb([P, NB, 4], F32, "G_kgs"); egl = kb.sb([P, NB, 4], F32, "G_egl")
    kb.op("act", lambda e: e.activation(beta[:], ab[:, :, 4:8], AF.Sigmoid), reads=[ab], writes=[beta])
    kb.op("dve", lambda e: e.tensor_tensor(sp_[:], ab[:, :, 0:4], dtb[:], ALU.add), reads=[ab, dtb], writes=[sp_])
    kb.op("act", lambda e: e.activation(sp_[:], sp_[:], AF.Exp), reads=[sp_], writes=[sp_])
    kb.op("act", lambda e: e.activation(sp_[:], sp_[:], AF.Ln, bias=1.0), reads=[sp_], writes=[sp_])
    kb.op("act", lambda e: e.activation(alog[:], alog[:], AF.Exp), reads=[alog], writes=[alog])
    kb.op("dve", lambda e: e.scalar_tensor_tensor(g[:], alog[:], -1.0, sp_[:], ALU.mult, ALU.mult), reads=[alog, sp_], writes=[g])
    psA = PS.f[5].view(PS.f[5][:, 0:NC4], "G_psA"); psB = PS.f[5].view(PS.f[5][:, 256:256 + NC4], "G_psB")
    gflat = g[:].rearrange("p t h -> p (t h)")
    kb.op("pe", lambda e: e.matmul(psA[:], UT, gflat, start=True, stop=True), reads=[gm, g], writes=[psA])
    kb.op("dve", lambda e: e.tensor_copy(gc[:].rearrange("p t h -> p (t h)"), psA[:]), reads=[psA], writes=[gc])
    kb.op("pe", lambda e: e.matmul(psB[:], SEL, gc[:].rearrange("p t h -> p (t h)"), start=True, stop=True), reads=[gm, gc], writes=[psB])
    kb.op("act", lambda e: e.activation(eg[:], gc[:], AF.Exp), reads=[gc], writes=[eg])
    kb.op("dve", lambda e: e.tensor_tensor(bg[:], beta[:], eg[:], ALU.mult), reads=[beta, eg], writes=[bg])
    kb.op("act", lambda e: e.activation(egl[:].rearrange("p t h -> p (t h)"), psB[:], AF.Exp), reads=[psB], writes=[egl])
    kb.op("dve", lambda e: e.tensor_tensor(kgs[:].rearrange("p t h -> p (t h)"), psB[:], gc[:].rearrange("p t h -> p (t h)"), ALU.subtract),
          reads=[psB, gc], writes=[kgs])
    kb.op("act", lambda e: e.activation(kgs[:], kgs[:], AF.Exp), reads=[kgs], writes=[kgs])
    xin = [kb.sb([P, 515], F32, f"G_xin{i}") for i in range(3)]
    cacc = [kb.sb([P, 512], F32, f"G_cacc{i}") for i in range(2)]
    csil = [kb.sb([P, 512], F32, f"G_csil{i}") for i in range(2)]
    csq = [kb.sb([P, 512], F32, f"G_csq{i}") for i in range(2)]
    crs = [kb.sb([P, 512], F32, f"G_crs{i}") for i in range(2)]
    qT = [[kb.sb([P, 512], BF16, f"G_qT{h}_{i}") for i in range(2)] for h in range(4)]
    kT = [[kb.sb([P, 512], BF16, f"G_kT{h}_{i}") for i in range(2)] for h in range(4)]
    vT = [[kb.sb([P, 512], BF16, f"G_vT{h}_{i}") for i in range(2)] for h in range(4)]
    S32 = [kb.sb([P, P], F32, f"G_S32_{h}") for h in range(4)]
    S16 = [kb.sb([P, P], BF16, f"G_S16_{h}") for h in range(4)]
    for h in range(4):
        kb.op("pool", lambda e: e.memset(S32[h][:], 0.0), writes=[S32[h]])
        kb.op("pool", lambda e: e.memset(S16[h][:], 0.0), writes=[S16[h]])

    def four(name, dt):
        return [kb.sb([P, P], dt, f"G_{name}{i}") for i in range(4)]
    GKb = four("GKb", BF16); Kg = four("Kg", BF16); Vb = four("Vb", BF16); gL = four("gL", F32)
    t1 = four("t1", F32); e1 = four("e1", F32); t2 = four("t2", F32); e2 = four("e2", F32)
    Acur = [four("Aa", F32), four("Ab", F32)]; Bcur = [four("Ba", F32), four("Bb", F32)]
    X = four("X", F32); X16 = four("X16", BF16); AttnT = four("AttnT", BF16)
    U = four("U", F32); WT = four("WT", BF16); Vn = four("Vn", BF16); qss = four("qss", F32); O = four("O", F32)
    junk = four("junk", F32); on = four("on", F32)
    rr = [kb.sb([P, 2], F32, f"G_rr{i}") for i in range(4)]
    zt = four("zt", F32); gz = four("gz", F32); yb = four("yb", BF16)
    yts = [[kb.sb([P, 512], BF16, f"G_yt{h}_{i}") for i in range(2)] for h in range(4)]
    SSB = PS.f[6]
    state = dict(nb=0)

    def pst(name):
        bank = state["nb"] % 6
        state["nb"] += 1
        return [PS.f32(bank, h, f"G_p{name}{h}") for h in range(4)]

    def each(fn):
        for h in range(4):
            fn(h)
    nx = 0; ncv = 0
    for tg in range(NG):
        par = tg % 2
        for h in range(4):
            for ti, (row0, dstl) in enumerate(((GD_Q0, qT), (GD_K0, kT), (GD_V0, vT))):
                xb = xin[nx % 3]; nx += 1
                r0 = row0 + h * P
                if tg == 0:
                    kb.op("pool", lambda e: e.memset(xb[:, 0:3], 0.0), writes=[xb])
                    kb.dma("sp", xb[:, 3:515], pf[r0:r0 + P, 0:512], xb, pf)
                else:
                    kb.dma("sp", xb[:], pf[r0:r0 + P, tg * 512 - 3:tg * 512 + 512], xb, pf)
                ca = cacc[ncv % 2]; cs = csil[ncv % 2]; s2 = csq[ncv % 2]; rs = crs[ncv % 2]; ncv += 1
                wb = ti * 16 + h * 4
                kb.op("dve", lambda e: e.tensor_scalar(ca[:], xb[:, 3:515], cw[:, wb + 3:wb + 4], None, ALU.mult), reads=[xb, cw], writes=[ca])
                for j in (2, 1, 0):
                    kb.op("dve", lambda e: e.scalar_tensor_tensor(ca[:], xb[:, j:j + 512], cw[:, wb + j:wb + j + 1], ca[:], ALU.mult, ALU.add),
                          reads=[xb, cw, ca], writes=[ca])
                dst = dstl[h][par]
                if ti == 2:
                    kb.op("act", lambda e: e.activation(dst[:], ca[:], AF.Silu), reads=[ca], writes=[dst])
                else:
                    kb.op("act", lambda e: e.activation(cs[:], ca[:], AF.Silu), reads=[ca], writes=[cs])
                    kb.op("act", lambda e: e.activation(s2[:], cs[:], AF.Square), reads=[cs], writes=[s2])
                    kb.op("pe", lambda e: e.matmul(SSB[:], ones[:], s2[:], start=True, stop=True), reads=[ones, s2], writes=[SSB], skip_self=True)
                    if ti == 0:
                        kb.op("act", lambda e: e.activation(rs[:], SSB[:], AF.Sqrt, bias=epst[:, 1:2], scale=128.0), reads=[SSB, epst], writes=[rs])
                    else:
                        kb.op("act", lambda e: e.activation(rs[:], SSB[:], AF.Sqrt, bias=epst[:, 0:1]), reads=[SSB, epst], writes=[rs])
                    kb.op("dve", lambda e: e.reciprocal(rs[:], rs[:]), reads=[rs], writes=[rs])
                    kb.op("dve", lambda e: e.tensor_tensor(dst[:], cs[:], rs[:], ALU.mult), reads=[cs, rs], writes=[dst])
        for c in range(4):
            tt = tg * 4 + c
            sl = slice(c * P, (c + 1) * P)
            sc = lambda t, h: t[:, tt, h:h + 1]
            kTh = [kT[h][par] for h in range(4)]; qTh = [qT[h][par] for h in range(4)]; vTh = [vT[h][par] for h in range(4)]
            each(lambda h: kb.dma("sp", zt[h][:], pt[tt * P:(tt + 1) * P, GD_Z0 + h * P:GD_Z0 + (h + 1) * P], zt[h], pt))
            each(lambda h: kb.op("act", lambda e: e.activation(gz[h][:], zt[h][:], AF.Silu), reads=[zt[h]], writes=[gz[h]]))
            each(lambda h: kb.op("pool", lambda e: e.tensor_tensor(gz[h][:], gz[h][:], gnb[:], ALU.mult), reads=[gz[h], gnb], writes=[gz[h]]))
            p_trK = [PS.bf(0, h, f"G_ptrK{h}") for h in range(4)]
            p_trV = [PS.bf(0, 4 + h, f"G_ptrV{h}") for h in range(4)]
            each(lambda h: kb.op("pe", lambda e: e.transpose(p_trK[h][:], kTh[h][:, sl], identb[:]), reads=[kTh[h], identb], writes=[p_trK[h]], skip_self=True))
            each(lambda h: kb.op("act", lambda e: e.activation(GKb[h][:], p_trK[h][:], AF.Copy, scale=sc(bg, h)), reads=[p_trK[h], bg], writes=[GKb[h]]))
            each(lambda h: kb.op("dve", lambda e: e.tensor_scalar(Kg[h][:], p_trK[h][:], sc(kgs, h), None, ALU.mult), reads=[p_trK[h], kgs], writes=[Kg[h]]))
            each(lambda h: kb.op("pe", lambda e: e.transpose(p_trV[h][:], vTh[h][:, sl], identb[:]), reads=[vTh[h], identb], writes=[p_trV[h]], skip_self=True))
            each(lambda h: kb.op("act", lambda e: e.activation(Vb[h][:], p_trV[h][:], AF.Copy, scale=sc(beta, h)), reads=[p_trV[h], beta], writes=[Vb[h]]))
            p_G = pst("G"); p_PT = pst("PT"); p_D = pst("D")
            each(lambda h: kb.op("pe", lambda e: e.matmul(p_G[h][:], kTh[h][:, sl], kTh[h][:, sl], start=True, stop=True), reads=[kTh[h]], writes=[p_G[h]], skip_self=True))
            each(lambda h: kb.op("pe", lambda e: e.matmul(p_PT[h][:], kTh[h][:, sl], qTh[h][:, sl], start=True, stop=True), reads=[kTh[h], qTh[h]], writes=[p_PT[h]], skip_self=True))
            each(lambda h: kb.op("dve", lambda e: e.tensor_scalar(gL[h][:], UT, sc(g, h), None, ALU.mult), reads=[gm, g], writes=[gL[h]]))

            def st_D(h):
                kb.op("pe", lambda e: e.matmul(p_D[h][:], gL[h][:], ones[:], start=True, stop=False), reads=[gL[h], ones], writes=[p_D[h]], inc=False, skip_self=True)
                kb.op("pe", lambda e: e.matmul(p_D[h][:], nones[:], gL[h][:], start=False, stop=True), reads=[gL[h], nones], writes=[p_D[h]], skip_self=True)
            each(st_D)
            each(lambda h: kb.op("dve", lambda e: e.tensor_tensor(t1[h][:], p_D[h][:], NEGL, ALU.add), reads=[p_D[h], gm], writes=[t1[h]]))
            each(lambda h: kb.op("act", lambda e: e.activation(e1[h][:], t1[h][:], AF.Exp), reads=[t1[h]], writes=[e1[h]]))
            each(lambda h: kb.op("dve", lambda e: e.scalar_tensor_tensor(t2[h][:], p_D[h][:], -1.0, NEGU, ALU.mult, ALU.add), reads=[p_D[h], gm], writes=[t2[h]]))
            each(lambda h: kb.op("act", lambda e: e.activation(e2[h][:], t2[h][:], AF.Exp), reads=[t2[h]], writes=[e2[h]]))
            A0 = Acur[0]; B0 = Bcur[0]
            each(lambda h: kb.op("dve", lambda e: e.scalar_tensor_tensor(A0[h][:], p_G[h][:], sc(beta, h), e1[h][:], ALU.mult, ALU.mult), reads=[p_G[h], beta, e1[h]], writes=[A0[h]]))
            each(lambda h: kb.op("dve", lambda e: e.tensor_tensor(AttnT[h][:], p_PT[h][:], e2[h][:], ALU.mult), reads=[p_PT[h], e2[h]], writes=[AttnT[h]]))
            p_Bt = pst("Bt")
            each(lambda h: kb.op("pe", lambda e: e.transpose(p_Bt[h][:], A0[h][:], identf[:]), reads=[A0[h], identf], writes=[p_Bt[h]], skip_self=True))
            each(lambda h: kb.op("act", lambda e: e.copy(B0[h][:], p_Bt[h][:]), reads=[p_Bt[h]], writes=[B0[h]]))
            each(lambda h: kb.op("dve", lambda e: e.tensor_tensor(X[h][:], identf[:], p_Bt[h][:], ALU.subtract), reads=[identf, p_Bt[h]], writes=[X[h]]))
            for lev in range(NLEV):
                Ac = Acur[lev % 2]; Bc = Bcur[lev % 2]
                An = Acur[(lev + 1) % 2]; Bn = Bcur[(lev + 1) % 2]
                last = (lev == NLEV - 1)
                p_A2 = pst("A2")
                each(lambda h: kb.op("pe", lambda e: e.matmul(p_A2[h][:], Bc[h][:], Ac[h][:], start=True, stop=True), reads=[Ac[h], Bc[h]], writes=[p_A2[h]], skip_self=True))
                if not last:
                    p_B2 = pst("B2")
                    each(lambda h: kb.op("pe", lambda e: e.matmul(p_B2[h][:], Ac[h][:], Bc[h][:], start=True, stop=True), reads=[Ac[h], Bc[h]], writes=[p_B2[h]], skip_self=True))
                each(lambda h: kb.op("act", lambda e: e.copy(An[h][:], p_A2[h][:]), reads=[p_A2[h]], writes=[An[h]]))
                if not last:
                    each(lambda h: kb.op("dve", lambda e: e.tensor_copy(Bn[h][:], p_B2[h][:]), reads=[p_B2[h]], writes=[Bn[h]]))
                p_Xn = pst("Xn")
                each(lambda h: kb.op("pe", lambda e: e.matmul(p_Xn[h][:], An[h][:], X[h][:], start=True, stop=True), reads=[An[h], X[h]], writes=[p_Xn[h]], skip_self=True))
                if not last:
                    each(lambda h: kb.op("dve", lambda e: e.tensor_tensor(X[h][:], X[h][:], p_Xn[h][:], ALU.add), reads=[X[h], p_Xn[h]], writes=[X[h]]))
                else:
                    each(lambda h: kb.op("dve", lambda e: e.tensor_tensor(X16[h][:], X[h][:], p_Xn[h][:], ALU.add), reads=[X[h], p_Xn[h]], writes=[X16[h]]))
            p_U = pst("U"); p_WT = pst("WT")
            each(lambda h: kb.op("pe", lambda e: e.matmul(p_U[h][:], X16[h][:], Vb[h][:], start=True, stop=True), reads=[X16[h], Vb[h]], writes=[p_U[h]], skip_self=True))
            each(lambda h: kb.op("pe", lambda e: e.matmul(p_WT[h][:], GKb[h][:], X16[h][:], start=True, stop=True), reads=[GKb[h], X16[h]], writes=[p_WT[h]], skip_self=True))
            each(lambda h: kb.op("act", lambda e: e.copy(U[h][:], p_U[h][:]), reads=[p_U[h]], writes=[U[h]]))
            each(lambda h: kb.op("act", lambda e: e.copy(WT[h][:], p_WT[h][:]), reads=[p_WT[h]], writes=[WT[h]]))
            p_WS = pst("WS"); p_QS = pst("QS")
            each(lambda h: kb.op("pe", lambda e: e.matmul(p_WS[h][:], WT[h][:], S16[h][:], start=True, stop=True), reads=[WT[h], S16[h]], writes=[p_WS[h]], skip_self=True))
            each(lambda h: kb.op("pe", lambda e: e.matmul(p_QS[h][:], qTh[h][:, sl], S16[h][:], start=True, stop=True), reads=[qTh[h], S16[h]], writes=[p_QS[h]], skip_self=True))
            each(lambda h: kb.op("dve", lambda e: e.tensor_tensor(Vn[h][:], U[h][:], p_WS[h][:], ALU.subtract), reads=[U[h], p_WS[h]], writes=[Vn[h]]))
            each(lambda h: kb.op("act", lambda e: e.activation(qss[h][:], p_QS[h][:], AF.Copy, scale=sc(eg, h)), reads=[p_QS[h], eg], writes=[qss[h]]))
            p_AV = pst("AV"); p_KV = pst("KV")
            each(lambda h: kb.op("pe", lambda e: e.matmul(p_AV[h][:], AttnT[h][:], Vn[h][:], start=True, stop=True), reads=[AttnT[h], Vn[h]], writes=[p_AV[h]], skip_self=True))
            each(lambda h: kb.op("pe", lambda e: e.matmul(p_KV[h][:], Kg[h][:], Vn[h][:], start=True, stop=True), reads=[Kg[h], Vn[h]], writes=[p_KV[h]], skip_self=True))
            each(lambda h: kb.op("dve", lambda e: e.tensor_tensor(O[h][:], p_AV[h][:], qss[h][:], ALU.add), reads=[p_AV[h], qss[h]], writes=[O[h]]))
            each(lambda h: kb.op("dve", lambda e: e.scalar_tensor_tensor(S32[h][:], S32[h][:], sc(egl, h), p_KV[h][:], ALU.mult, ALU.add),
                                 reads=[S32[h], egl, p_KV[h]], writes=[S32[h]]))
            each(lambda h: kb.op("act", lambda e: e.copy(S16[h][:], S32[h][:]), reads=[S32[h]], writes=[S16[h]]))
            each(lambda h: kb.op("act", lambda e: e.activation(junk[h][:], O[h][:], AF.Square, accum_out=rr[h][:, 0:1]), reads=[O[h], rr[h]], writes=[junk[h], rr[h]]))
            each(lambda h: kb.op("act", lambda e: e.activation(rr[h][:, 1:2], rr[h][:, 0:1], AF.Sqrt, bias=epst[:, 0:1], scale=1.0 / 128.0), reads=[rr[h], epst], writes=[rr[h]]))
            each(lambda h: kb.op("dve", lambda e: e.reciprocal(rr[h][:, 1:2], rr[h][:, 1:2]), reads=[rr[h]], writes=[rr[h]]))
            each(lambda h: kb.op("dve", lambda e: e.tensor_scalar(on[h][:], O[h][:], rr[h][:, 1:2], None, ALU.mult), reads=[O[h], rr[h]], writes=[on[h]]))
            each(lambda h: kb.op("pool", lambda e: e.tensor_tensor(yb[h][:], on[h][:], gz[h][:], ALU.mult), reads=[on[h], gz[h]], writes=[yb[h]]))
            p_tl = [PS.bf(0, h, f"G_ptl{h}") for h in range(4)]
            each(lambda h: kb.op("pe", lambda e: e.transpose(p_tl[h][:], yb[h][:], identb[:]), reads=[yb[h], identb], writes=[p_tl[h]], skip_self=True))
            each(lambda h: kb.op("act", lambda e: e.copy(yts[h][par][:, sl], p_tl[h][:]), reads=[p_tl[h]], writes=[yts[h][par]]))
        for h in range(4):
            r0 = ybase + h * P
            kb.dma("sp", yT[r0:r0 + P, tg * 512:(tg + 1) * 512], yts[h][par][:], yT, yts[h][par])
    kb.barrier([yT])
    kb.release(m)


def gdn_masks():
    import numpy as np
    t = np.arange(128)
    UT = (t[:, None] <= t[None, :]).astype(np.float32)
    SEL = np.zeros((128, 128), np.float32); SEL[127, :] = 1.0
    NEGL = np.where(t[:, None] > t[None, :], 0.0, NEGM).astype(np.float32)
    NEGU = np.where(t[None, :] >= t[:, None], 0.0, NEGM).astype(np.float32)
    return np.stack([UT, SEL, NEGL, NEGU])


import numpy as _np
from concourse.bass_utils import run_bass_kernel_spmd

D_MODEL = 4096
SEQ = 4096
BATCH = 2
DEPTH = 2
NF = 3584
NT = 2568
TQ = 1024
NCORES = 8
_OFF = {}
_sizes = (2048, 2048, 2048, 2048, 2048, 2048, 2048, 2048, 16, 16, 2048, 2048, 2048, 2048, 3 * 4096)
_names = ("rq", "rk", "rv", "rz", "gq", "gk", "gv", "gz", "ga", "gb", "dq", "dk", "dv", "dz", "gate")
_o = 0
for _n, _s in zip(_names, _sizes):
    _OFF[_n] = _o
    _o += _s


def my_cols(g):
    r = lambda name, w=512: _np.arange(_OFF[name] + g * w, _OFF[name] + (g + 1) * w)
    feat = [r("rq"), r("rk"), r("dq"), r("dk"), r("gq"), r("gk"), r("gv")]
    tok = [r("rv"), r("rz"), r("dv"), r("dz"), r("gz"), r("ga", 4), r("gb", 4)]
    return _np.concatenate(feat + tok)


def build_L1():
    nc = bass.Bass("TRN2", target_bir_lowering=False)
    kb = KB(nc)
    xT = kb.dram("xT", [D_MODEL, TQ], F32, kind="ExternalInput")
    g = kb.dram("gcol", [128, D_MODEL // 128], F32, kind="ExternalInput")
    hT = kb.dram("hT", [D_MODEL, TQ], BF16, kind="ExternalOutput")
    gs = kb.sb([128, D_MODEL // 128], F32, "gain")
    kb.dma("sp", gs[:], g[:], gs, g)
    phaseA(kb, xT, gs, hT, D_MODEL, TQ)
    kb.finish([hT]); kb.close()
    return nc


L2_INPUTS = dict(cs=[2, 128, SEQ], m0=[2, 128, 512], md=[2, 4, 128, 512], cf=[2, 128, SEQ // 128], gnbr=[128, 512],
                 bn=[2, 5, 128, 512], fb=[128, 2], gqk=[128, 2], lamv=[4, 128, 128], subb=[128, 256],
                 cw=[128, 48], dtb=[128, SEQ // 128 * 4], alog=[128, SEQ // 128 * 4], gnbg=[128, 128], gm=[4, 128, 128])


def emit_L2(kb, hT_parts, W, C, yT, layer_idx, S=SEQ, D=D_MODEL):
    pf = kb.dram(f"pf{layer_idx}", [NF, S], F32)
    pt = kb.dram(f"pt{layer_idx}", [S, NT], F32)
    phaseB1(kb, hT_parts, TQ, W, pf, pt, D, S, NF, NT)
    ret_mixer(kb, pf, pt, C["cs"], C["m0"], C["md"], C["cf"], C["gnbr"], yT, S, ybase=0)
    gdn_mixer(kb, pf, pt, C["cw"], C["dtb"], C["alog"], C["gnbg"], C["gm"], yT, S, ybase=512)
    diff_mixer(kb, pf, pt, C["bn"], C["fb"], C["gqk"], C["lamv"], C["subb"], yT, S, layer_idx, ybase=1024)


def build_L2(layer_idx):
    nc = bass.Bass("TRN2", target_bir_lowering=False)
    kb = KB(nc)
    hT_all = kb.dram("hT_all", [4 * D_MODEL, TQ], BF16, kind="ExternalInput")
    parts = [hT_all.view(hT_all[r * D_MODEL:(r + 1) * D_MODEL, :], f"hTp{r}") for r in range(4)]
    W = kb.dram("W", [D_MODEL, NF + NT], F32, kind="ExternalInput")
    C = {k: kb.dram(k, shp, F32, kind="ExternalInput") for k, shp in L2_INPUTS.items()}
    yT = kb.dram("yT", [1536, SEQ], BF16, kind="ExternalOutput")
    emit_L2(kb, parts, W, C, yT, layer_idx)
    kb.finish([yT]); kb.close()
    return nc


def build_L3():
    nc = bass.Bass("TRN2", target_bir_lowering=False)
    kb = KB(nc)
    hT = kb.dram("hT", [D_MODEL, TQ], BF16, kind="ExternalInput")
    yT = kb.dram("yT_all", [3 * 2048, TQ], BF16, kind="ExternalInput")
    xT = kb.dram("xT", [D_MODEL, TQ], F32, kind="ExternalInput")
    Wg = kb.dram("Wg", [D_MODEL, 3 * D_MODEL], F32, kind="ExternalInput")
    Wb = kb.dram("Wb", [3 * 2048, D_MODEL], F32, kind="ExternalInput")
    Wo = kb.dram("Wo", [D_MODEL, D_MODEL], F32, kind="ExternalInput")
    xo = kb.dram("xo", [D_MODEL, TQ], F32, kind="ExternalOutput")
    phaseC(kb, hT, yT, xT, Wg, Wb, Wo, xo, D_MODEL, TQ)
    kb.finish([xo]); kb.close()
    return nc


_DIFF_IDX = None


def layer_consts(inp, l, g):
    global _DIFF_IDX
    S = SEQ
    NB = S // 128
    rep = lambda v, n=128: _np.ascontiguousarray(_np.broadcast_to(_np.asarray(v, _np.float32).reshape(1, -1), (n, _np.asarray(v).size)))
    c = {}
    cs, m0, md, cf = ret_consts(S, [2 * g, 2 * g + 1])
    c.update(cs=cs, m0=m0, md=md, cf=cf)
    c["gnbr"] = rep(inp["ret_gn_gain"][l][g * 512:(g + 1) * 512])
    if _DIFF_IDX is None:
        _DIFF_IDX = diff_bias_index()
    idx, vis = _DIFF_IDX
    rb = _np.asarray(inp["rel_bias"], _np.float32)
    bn = _np.empty((2, 5, 128, 512), _np.float32)
    for a in range(2):
        bn[a] = _np.where(vis, rb[idx, 2 * g + a], _np.float32(NEG))
    c["bn"] = bn
    c["fb"] = rep(rb[15, 2 * g:2 * g + 2])
    c["gqk"] = _np.ascontiguousarray(_np.stack([inp["diff_q_gain"][l], inp["diff_k_gain"][l]], 1).astype(_np.float32))
    c["lamv"] = _np.stack([rep(inp[k][l]) for k in ("diff_lambda_q1", "diff_lambda_k1", "diff_lambda_q2", "diff_lambda_k2")])
    c["subb"] = rep(inp["diff_subln_gain"][l])
    cwl = _np.asarray(inp["gdn_conv_w"][l], _np.float32)
    cw = _np.empty((128, 3, 4, 4), _np.float32)
    for t in range(3):
        for i in range(4):
            h = 4 * g + i
            cw[:, t, i, :] = cwl[:, t * 2048 + h * 128:t * 2048 + (h + 1) * 128].T
    c["cw"] = cw.reshape(128, 48)
    c["dtb"] = rep(_np.tile(_np.asarray(inp["gdn_dt_bias"][l], _np.float32)[4 * g:4 * g + 4], NB))
    c["alog"] = rep(_np.tile(_np.asarray(inp["gdn_a_log"][l], _np.float32)[4 * g:4 * g + 4], NB))
    c["gnbg"] = rep(inp["gdn_norm_gain"][l])
    c["gm"] = gdn_masks()
    return c


_PROGS = {}


def _prog(key, fn):
    if key not in _PROGS:
        _PROGS[key] = fn()
    return _PROGS[key]


def kernel(**inp):
    inp = {k: _np.asarray(v) for k, v in inp.items()}
    x = inp["x"].astype(_np.float32)
    cores = list(range(NCORES))
    xT = [_np.ascontiguousarray(x[c // 4, (c % 4) * TQ:((c % 4) + 1) * TQ, :].T) for c in cores]
    for l in range(DEPTH):
        gcol = _np.ascontiguousarray(inp["norm_gain"][l].astype(_np.float32).reshape(D_MODEL // 128, 128).T)
        r1 = run_bass_kernel_spmd(_prog("L1", build_L1), [{"xT": xT[c], "gcol": gcol} for c in cores], core_ids=cores).results
        hT = [r1[c]["hT"] for c in cores]
        w_in = inp["w_in"][l]
        in2 = []
        for c in cores:
            b, g = c // 4, c % 4
            m = {"hT_all": _np.concatenate([hT[b * 4 + r] for r in range(4)], axis=0),
                 "W": _np.ascontiguousarray(w_in[:, my_cols(g)], dtype=_np.float32)}
            m.update(layer_consts(inp, l, g))
            in2.append(m)
        r2 = run_bass_kernel_spmd(_prog(("L2", l), lambda: build_L2(l)), in2, core_ids=cores).results
        del in2
        yT = [r2[c]["yT"] for c in cores]
        Wg = _np.ascontiguousarray(w_in[:, _OFF["gate"]:], dtype=_np.float32)
        Wb = _np.ascontiguousarray(inp["w_branch"][l].reshape(3 * 2048, D_MODEL), dtype=_np.float32)
        Wo = _np.ascontiguousarray(inp["w_out"][l], dtype=_np.float32)
        in3 = []
        for c in cores:
            b, g = c // 4, c % 4
            ya = _np.empty((3, 4, 512, TQ), yT[0].dtype)
            for n in range(3):
                for r in range(4):
                    ya[n, r] = yT[b * 4 + r][n * 512:(n + 1) * 512, g * TQ:(g + 1) * TQ]
            in3.append({"hT": hT[c], "yT_all": ya.reshape(3 * 2048, TQ), "xT": xT[c], "Wg": Wg, "Wb": Wb, "Wo": Wo})
        r3 = run_bass_kernel_spmd(_prog("L3", build_L3), in3, core_ids=cores).results
        del in3
        xT = [r3[c]["xo"] for c in cores]
    out = _np.empty((BATCH, SEQ, D_MODEL), _np.float32)
    for c in cores:
        out[c // 4, (c % 4) * TQ:((c % 4) + 1) * TQ, :] = xT[c].T
    return out


NCOLS = NF + NT
FUSED_CONSTS = dict(cs=[2, 128, SEQ], m0=[4, 2, 128, 512], md=[4, 2, 4, 128, 512], cf=[4, 2, 128, SEQ // 128],
                    gnbr=[DEPTH, 4, 128, 512], bn=[4, 2, 5, 128, 512], fb=[4, 128, 2], gqk=[DEPTH, 128, 2],
                    lamv=[DEPTH, 4, 128, 128], subb=[DEPTH, 128, 256], cw=[DEPTH, 4, 128, 48],
                    dtb=[DEPTH, 4, 128, SEQ // 128 * 4], alog=[DEPTH, 4, 128, SEQ // 128 * 4], gnbg=[DEPTH, 128, 128],
                    gm=[4, 128, 128])
PER_LAYER = ("gnbr", "gqk", "lamv", "subb", "cw", "dtb", "alog", "gnbg")
PER_GROUP = ("m0", "md", "cf", "gnbr", "bn", "fb", "cw", "dtb", "alog")


DBG = dict(layers=DEPTH, groups=(0, 1, 2, 3), mixers=('r', 'g', 'd'), A=True, B=True, C=True)


def build_fused():
    nc = bass.Bass("TRN2", target_bir_lowering=False)
    kb = KB(nc)
    D, S = D_MODEL, SEQ
    xT = kb.dram("xT", [D, S], F32, kind="ExternalInput")
    gcol = kb.dram("gcol", [128, DEPTH * (D // 128)], F32, kind="ExternalInput")
    W = [kb.dram(f"W{l}", [D, 4 * NCOLS], F32, kind="ExternalInput") for l in range(DEPTH)]
    Wg = [kb.dram(f"Wg{l}", [D, 3 * D], F32, kind="ExternalInput") for l in range(DEPTH)]
    Wb = [kb.dram(f"Wb{l}", [3 * 2048, D], F32, kind="ExternalInput") for l in range(DEPTH)]
    Wo = [kb.dram(f"Wo{l}", [D, D], F32, kind="ExternalInput") for l in range(DEPTH)]
    C = {k: kb.dram(k, shp, F32, kind="ExternalInput") for k, shp in FUSED_CONSTS.items()}
    xo = kb.dram("xo", [D, S], F32, kind="ExternalOutput")
    hT = kb.dram("hT_s", [D, S], BF16)
    pf = kb.dram("pf_s", [NF, S], F32)
    pt = kb.dram("pt_s", [S, NT], F32)
    yT = kb.dram("yT_s", [3 * 2048, S], BF16)
    x1 = kb.dram("x1_s", [D, S], F32)
    wscr = (kb.dram("wg16_s", [3 * (D // 128), 128, D], BF16), kb.dram("wb16_s", [3 * (D // 128), 128, 2048], BF16),
            kb.dram("wo16_s", [D // 128, 128, D], BF16))
    gs = kb.sb([128, DEPTH * (D // 128)], F32, "gain")
    kb.dma("sp", gs[:], gcol[:], gs, gcol)
    for l in range(DBG['layers']):
        xin = xT if l == 0 else x1
        xout = x1 if l < DBG['layers'] - 1 else xo
        gl = gs.view(gs[:, l * (D // 128):(l + 1) * (D // 128)], f"gain{l}")
        if DBG['A']:
            phaseA(kb, xin, gl, hT, D, S)
        for gg in DBG['groups']:
            Cg = {}
            for k, buf in C.items():
                ap = buf.t
                if k in PER_LAYER:
                    ap = ap[l]
                if k in PER_GROUP:
                    ap = ap[gg]
                Cg[k] = buf.view(ap, f"{k}_{l}_{gg}")
            if DBG['B']:
                phaseB1(kb, [hT], S, W[l], pf, pt, D, S, NF, NT, wcol0=gg * NCOLS)
            if 'r' in DBG['mixers']:
              ret_mixer(kb, pf, pt, Cg["cs"], Cg["m0"], Cg["md"], Cg["cf"], Cg["gnbr"], yT, S, ybase=0 * 2048 + gg * 512)
            if 'g' in DBG['mixers']:
              gdn_mixer(kb, pf, pt, Cg["cw"], Cg["dtb"], Cg["alog"], Cg["gnbg"], Cg["gm"], yT, S, ybase=1 * 2048 + gg * 512)
            if 'd' in DBG['mixers']:
              diff_mixer(kb, pf, pt, Cg["bn"], Cg["fb"], Cg["gqk"], Cg["lamv"], Cg["subb"], yT, S, l, ybase=2 * 2048 + gg * 512)
        if DBG['C']:
            phaseC(kb, hT, yT, xin, Wg[l], Wb[l], Wo[l], xout, D, S, scratch=wscr)
    kb.finish([xo]); kb.close()
    return nc


def kernel_fused(inp):
    x = inp["x"].astype(_np.float32)
    cores = list(range(NCORES))
    shared = {}
    shared["gcol"] = _np.ascontiguousarray(_np.concatenate(
        [inp["norm_gain"][l].astype(_np.float32).reshape(D_MODEL // 128, 128).T for l in range(DEPTH)], axis=1))
    allcols = _np.concatenate([my_cols(g) for g in range(4)])
    for l in range(DEPTH):
        w_in = inp["w_in"][l]
        shared[f"W{l}"] = _np.ascontiguousarray(w_in[:, allcols], dtype=_np.float32)
        shared[f"Wg{l}"] = _np.ascontiguousarray(w_in[:, _OFF["gate"]:], dtype=_np.float32)
        shared[f"Wb{l}"] = _np.ascontiguousarray(inp["w_branch"][l].reshape(3 * 2048, D_MODEL), dtype=_np.float32)
        shared[f"Wo{l}"] = _np.ascontiguousarray(inp["w_out"][l], dtype=_np.float32)
    per = [[layer_consts(inp, l, g) for g in range(4)] for l in range(DEPTH)]
    for k in FUSED_CONSTS:
        if k in PER_LAYER and k in PER_GROUP:
            shared[k] = _np.stack([_np.stack([per[l][g][k] for g in range(4)]) for l in range(DEPTH)])
        elif k in PER_LAYER:
            shared[k] = _np.stack([per[l][0][k] for l in range(DEPTH)])
        elif k in PER_GROUP:
            shared[k] = _np.stack([per[0][g][k] for g in range(4)])
        else:
            shared[k] = per[0][0][k]
        shared[k] = _np.ascontiguousarray(shared[k], dtype=_np.float32)
    xTb = [_np.ascontiguousarray(x[b].T) for b in range(BATCH)]
    in_maps = []
    for c in cores:
        m = dict(shared)
        m["xT"] = xTb[c // 4]
        in_maps.append(m)
    res = run_bass_kernel_spmd(_prog("fused", build_fused), in_maps, core_ids=cores).results
    out = _np.empty((BATCH, SEQ, D_MODEL), _np.float32)
    for c in cores:
        b, g = c // 4, c % 4
        out[b, g * TQ:(g + 1) * TQ, :] = res[c]["xo"][:, g * TQ:(g + 1) * TQ].T
    return out


FUSED = True
_kernel_unfused = kernel


def kernel(**inp):
    if FUSED:
        return kernel_fused({k: _np.asarray(v) for k, v in inp.items()})
    return _kernel_unfused(**inp)
```

```python
import numpy as np
import concourse.bass as bass
import concourse.mybir as mybir

F32 = mybir.dt.float32
BF16 = mybir.dt.bfloat16
AF = mybir.ActivationFunctionType
ALU = mybir.AluOpType
AX = mybir.AxisListType


class Buf:
    __slots__ = ("t", "name", "w", "r", "dsem", "dcount", "root", "excl", "persist")

    def __init__(self, t, name, root=None, excl=False):
        self.t = t
        self.name = name
        self.root = root if root is not None else self
        self.excl = excl
        self.persist = False
        self.w = {}
        self.r = {}
        self.dsem = None
        self.dcount = 0

    def __getitem__(self, idx):
        return self.t[idx]

    def view(self, ap, name):
        return Buf(ap, name, root=self.root)


class KB:
    def __init__(self, nc, same_engine_sync=True):
        self.nc = nc
        self.stack = []
        self.engs = {}
        self.same_engine_sync = same_engine_sync
        for name, e in (("pe", nc.tensor), ("act", nc.scalar), ("dve", nc.vector),
                        ("pool", nc.gpsimd), ("sp", nc.sync)):
            sem = self._sem("s_" + name)
            self.engs[name] = dict(e=e, sem=sem, cnt=0, seen={}, name=name)
        self.nbuf = 0
        self.ninst = 0
        self.bufs = []
        self.pstack = []
        self.semcnt = {}

    def _sem(self, name, persistent=False):
        g = self.nc.semaphore(name)
        s = g.__enter__()
        (self.pstack if persistent else self.stack).append(g)
        return s

    def sb(self, shape, dt, name=None):
        self.nbuf += 1
        name = f"{name or 'sb'}_{self.nbuf}"
        g = self.nc.sbuf_tensor(name, list(shape), dt)
        t = g.__enter__()
        self.stack.append(g)
        b = Buf(t, name)
        self.bufs.append(b)
        return b

    def ps(self, shape, dt=F32, name=None):
        self.nbuf += 1
        name = f"{name or 'ps'}_{self.nbuf}"
        g = self.nc.psum_tensor(name, list(shape), dt)
        t = g.__enter__()
        self.stack.append(g)
        b = Buf(t, name)
        self.bufs.append(b)
        return b

    def dram(self, name, shape, dt, kind="Internal"):
        t = self.nc.dram_tensor(name, list(shape), dt, kind=kind)
        b = Buf(t.ap(), name)
        b.persist = True
        self.bufs.append(b)
        return b

    def close(self):
        for g in reversed(self.stack):
            g.__exit__(None, None, None)
        self.stack = []
        for g in reversed(self.pstack):
            g.__exit__(None, None, None)
        self.pstack = []

    def _wait(self, E, sem, val):
        key = sem.num
        if E["seen"].get(key, 0) >= val:
            return
        E["seen"][key] = val
        E["e"].wait_ge(sem, val)

    def _split(self, reads, writes):
        rr = []; ww = []
        for b in writes:
            ww.append(b.root)
        for b in reads:
            b = b.root
            (ww if b.excl else rr).append(b)
        return rr, ww

    def _deps(self, E, reads, writes, skip_self=False):
        own = E["sem"].num
        for b in reads:
            for k, (s, v) in b.w.items():
                if k == own and (skip_self or not self.same_engine_sync):
                    continue
                self._wait(E, s, v)
        for b in writes:
            for k, (s, v) in list(b.w.items()) + list(b.r.items()):
                if k == own and (skip_self or not self.same_engine_sync):
                    continue
                self._wait(E, s, v)

    def _record(self, sem, val, reads, writes):
        k = sem.num
        for b in reads:
            b.r[k] = (sem, val)
        for b in writes:
            b.w = {k: (sem, val)}
            b.r = {}

    def op(self, eng, fn, reads=(), writes=(), inc=True, skip_self=False):
        E = self.engs[eng]
        reads, writes = self._split(reads, writes)
        self._deps(E, reads, writes, skip_self=skip_self)
        ins = fn(E["e"])
        self.ninst += 1
        if inc:
            E["cnt"] += 1
            ins.then_inc(E["sem"], 1)
            self._record(E["sem"], E["cnt"], reads, writes)
        else:
            self._record(E["sem"], E["cnt"] + 1, reads, writes)
        return ins

    def dma(self, q, out_ap, in_ap, dst: Buf, src: Buf, **kw):
        E = self.engs[q]
        dst = dst.root; src = src.root
        if dst.dsem is None:
            dst.dsem = self._sem("d_" + dst.name, persistent=dst.persist)
            dst.dcount = self.semcnt.get(dst.dsem.num, 0)
        dk = dst.dsem.num
        for k_, (s_, v_) in src.w.items():
            self._wait(E, s_, v_)
        for k_, (s_, v_) in list(dst.w.items()) + list(dst.r.items()):
            if k_ == dk:
                continue
            self._wait(E, s_, v_)
        ins = E["e"].dma_start(out=out_ap, in_=in_ap, **kw)
        dst.dcount += 16
        self.semcnt[dst.dsem.num] = dst.dcount
        ins.then_inc(dst.dsem, 16)
        self.ninst += 1
        k = dst.dsem.num
        src.r[k] = (dst.dsem, dst.dcount)
        dst.w = {k: (dst.dsem, dst.dcount)}
        dst.r = {}
        return ins

    def finish(self, bufs):
        E = self.engs["sp"]
        for b in bufs:
            for k, (s, v) in b.w.items():
                self._wait(E, s, v)

    def mark(self):
        return (len(self.stack), len(self.bufs))

    def release(self, mark):
        ns, nb = mark
        while len(self.stack) > ns:
            g = self.stack.pop()
            g.__exit__(None, None, None)
        del self.bufs[nb:]

    def barrier(self, dma_bufs=()):
        for en, E in self.engs.items():
            for fn, F in self.engs.items():
                if fn == en or F["cnt"] == 0:
                    continue
                self._wait(E, F["sem"], F["cnt"])
            for b in self.bufs:
                if b.dsem is not None:
                    self._wait(E, b.dsem, b.dcount)
        for b in self.bufs:
            b.w = {}
            b.r = {}

    def collective(self, kind, dst: Buf, src: Buf, groups):
        E = self.engs["pool"]
        for k_, (s_, v_) in src.w.items():
            self._wait(E, s_, v_)
        for k_, (s_, v_) in list(dst.w.items()) + list(dst.r.items()):
            self._wait(E, s_, v_)
        E["e"].collective_compute(kind, ALU.bypass, replica_groups=groups, ins=[src.t], outs=[dst.t])
        self.ninst += 1
        if not hasattr(self, "_ccd"):
            self._ccd = self.sb([128, 8], F32, "cc_dummy")
        d = self._ccd
        ins = E["e"].memset(d[:], 0.0)
        E["cnt"] += 1
        ins.then_inc(E["sem"], 1)
        k = id(E["sem"])
        src.r[k] = (E["sem"], E["cnt"])
        dst.w = {k: (E["sem"], E["cnt"])}
        dst.r = {}
        return ins


P = 128
EPS = 1e-6


def cdiv(a, b):
    return (a + b - 1) // b


def phaseA(kb, xT, gain_sb, hT, D, T):
    KC = D // P
    G4 = 4
    m = kb.mark()
    ones = kb.sb([P, P], F32, "A_ones")
    kb.op("pool", lambda e: e.memset(ones[:], 1.0), writes=[ones])
    epst = kb.sb([P, 1], F32, "A_eps")
    kb.op("pool", lambda e: e.memset(epst[:], EPS), writes=[epst])
    xs = [kb.sb([P, G4, 512], F32, f"A_x{i}") for i in range(2)]
    sq = [kb.sb([P, 512], F32, f"A_sq{i}") for i in range(2)]
    hs = [kb.sb([P, G4, 512], BF16, f"A_h{i}") for i in range(2)]
    ss = kb.ps([P, 512], F32, "A_ss")
    rstd = kb.sb([P, 512], F32, "A_rstd")
    xv = xT.t.rearrange("(c p) t -> p c t", p=P)
    hv = hT.t.rearrange("(c p) t -> p c t", p=P)
    n = 0
    for tg in range(T // 512):
        tsl = slice(tg * 512, (tg + 1) * 512)
        for c4 in range(KC // G4):
            xb = xs[n % 2]; n += 1
            kb.dma("sp", xb[:], xv[:, c4 * G4:(c4 + 1) * G4, tsl], xb, xT)
            for i in range(G4):
                kc = c4 * G4 + i
                s = sq[kc % 2]
                kb.op("act", lambda e: e.activation(s[:], xb[:, i, :], AF.Square), reads=[xb], writes=[s])
                kb.op("pe", lambda e: e.matmul(ss[:], ones[:], s[:], start=(kc == 0), stop=(kc == KC - 1)),
                      reads=[ones, s], writes=[ss], skip_self=True)
        kb.op("act", lambda e: e.activation(rstd[:], ss[:], AF.Sqrt, bias=epst[:], scale=1.0 / D),
              reads=[ss, epst], writes=[rstd])
        kb.op("dve", lambda e: e.reciprocal(rstd[:], rstd[:]), reads=[rstd], writes=[rstd])
        for c4 in range(KC // G4):
            xb = xs[n % 2]; hb = hs[n % 2]; n += 1
            kb.dma("sp", xb[:], xv[:, c4 * G4:(c4 + 1) * G4, tsl], xb, xT)
            for i in range(G4):
                kc = c4 * G4 + i
                kb.op("dve", lambda e: e.scalar_tensor_tensor(hb[:, i, :], xb[:, i, :], gain_sb[:, kc:kc + 1], rstd[:],
                                                              ALU.mult, ALU.mult),
                      reads=[xb, gain_sb, rstd], writes=[hb])
            kb.dma("sp", hv[:, c4 * G4:(c4 + 1) * G4, tsl], hb[:], hT, hb)
    kb.barrier([hT])
    kb.release(m)


def load_kc_tile(kb, q, dst, dst_ap_fn, src, src_view, KC, nsplit=4):
    step = cdiv(KC, nsplit)
    for c0 in range(0, KC, step):
        c1 = min(KC, c0 + step)
        kb.dma(q, dst_ap_fn(c0, c1), src_view[:, c0:c1, :], dst, src)


def wcast_units(kb, Wg, Wb, Wo, scratch, stg, D, BW=2048):
    KC = D // P
    KBW = BW // P
    Wg16, Wb16, Wo16 = scratch
    wgs, wbs = stg
    Wgv = Wg.t.rearrange("(c p) n -> p c n", p=P)
    Wbv = Wb.t.rearrange("(c p) n -> p c n", p=P)
    Wov = Wo.t.rearrange("(c p) n -> p c n", p=P)
    pend = None
    pu = 0
    units = [("gb", dc, n) for dc in range(KC) for n in range(3)] + [("o", ec, 0) for ec in range(KC)]
    for (kind, a, n) in units:
        wg = wgs[pu % 2]; wb = wbs[pu % 2]; pu += 1
        if kind == "gb":
            c0 = n * D + a * P
            load_kc_tile(kb, "pool", wg, lambda x, y: wg[:, x:y, :], Wg, Wgv[:, :, c0:c0 + P], KC, nsplit=2)
            load_kc_tile(kb, "pool", wb, lambda x, y: wb[:, x:y, :], Wb, Wbv[:, n * KBW:(n + 1) * KBW, a * P:(a + 1) * P], KBW, nsplit=1)
            cur = [(Wg16[a * 3 + n].rearrange("p (k c) -> p k c", c=P), wg, Wg16), (Wb16[a * 3 + n].rearrange("p (k c) -> p k c", c=P), wb, Wb16)]
        else:
            load_kc_tile(kb, "pool", wg, lambda x, y: wg[:, x:y, :], Wo, Wov[:, :, a * P:(a + 1) * P], KC, nsplit=2)
            cur = [(Wo16[a].rearrange("p (k c) -> p k c", c=P), wg, Wo16)]
        if pend is not None:
            for (dst_ap, sbuf, dbuf) in pend:
                kb.dma("pool", dst_ap, sbuf[:], dbuf, sbuf)
        pend = cur
        yield
    for (dst_ap, sbuf, dbuf) in pend:
        kb.dma("pool", dst_ap, sbuf[:], dbuf, sbuf)
    yield


def phaseB1(kb, hT_parts, Tq, W, pf, pt, D, S, NF, NT, wcol0=0, pump=None):
    KC = D // P
    m = kb.mark()
    wts = [kb.sb([P, KC, 512], BF16, f"B_w{i}") for i in range(2)]
    hts = [kb.sb([P, KC, 512], BF16, f"B_h{i}") for i in range(2)]
    osts = [kb.sb([P, 512], F32, f"B_o{i}") for i in range(4)]
    pss = [kb.ps([P, 512], F32, f"B_ps{i}") for i in range(4)]
    Wv = W.t.rearrange("(c p) n -> p c n", p=P)
    hviews = [h.t.rearrange("(c p) t -> p c t", p=P) for h in hT_parts]
    NTG = S // 512
    nfb = NF // 512
    ntb = NT // 512
    rem = NT - ntb * 512
    blocks = [("f", i * 512, 512) for i in range(nfb)] + [("t", NF + i * 512, 512) for i in range(ntb)]
    if rem:
        blocks.append(("t", NF + ntb * 512, rem))
    Wv = Wv[:, :, wcol0:wcol0 + NF + NT]
    nw = 0; k = 0
    iters = [(bi, tg) for bi in range(len(blocks)) for tg in range(NTG)]

    def load_h(i):
        _, tg_ = iters[i]
        ht_ = hts[i % 2]
        part = (tg_ * 512) // Tq
        off = tg_ * 512 - part * Tq
        load_kc_tile(kb, "sp", ht_, lambda a, b: ht_[:, a:b, :], hT_parts[part],
                     hviews[part][:, :, off:off + 512], KC)
    load_h(0)
    wt = None
    for it, (bi, tg) in enumerate(iters):
        kind, c0, cw = blocks[bi]
        if tg == 0:
            wt = wts[nw % 2]; nw += 1
            load_kc_tile(kb, "pool", wt, lambda a, b: wt[:, a:b, 0:cw], W, Wv[:, :, c0:c0 + cw], KC)
        if it + 1 < len(iters):
            load_h(it + 1)
        if pump is not None and it % 2 == 1:
            next(pump, None)
        ht = hts[it % 2]
        if True:
            for sub in range(4):
                if kind == "f" and sub * 128 >= cw:
                    continue
                ps = pss[k % 4]; ost = osts[k % 4]
                for kc in range(KC):
                    if kind == "f":
                        kb.op("pe", lambda e: e.matmul(ps[:], wt[:, kc, sub * 128:(sub + 1) * 128], ht[:, kc, :],
                                                       start=(kc == 0), stop=(kc == KC - 1)),
                              reads=[wt, ht], writes=[ps], inc=(kc == KC - 1), skip_self=True)
                    else:
                        kb.op("pe", lambda e: e.matmul(ps[:, 0:cw], ht[:, kc, sub * 128:(sub + 1) * 128], wt[:, kc, 0:cw],
                                                       start=(kc == 0), stop=(kc == KC - 1)),
                              reads=[wt, ht], writes=[ps], inc=(kc == KC - 1), skip_self=True)
                ww = 512 if kind == "f" else cw
                if k % 2 == 0:
                    kb.op("act", lambda e: e.copy(ost[:, 0:ww], ps[:, 0:ww]), reads=[ps], writes=[ost])
                else:
                    kb.op("dve", lambda e: e.tensor_copy(ost[:, 0:ww], ps[:, 0:ww]), reads=[ps], writes=[ost])
                if kind == "f":
                    kb.dma("sp", pf[c0 + sub * 128:c0 + (sub + 1) * 128, tg * 512:(tg + 1) * 512], ost[:], pf, ost)
                else:
                    r0 = tg * 512 + sub * 128
                    kb.dma("sp", pt[r0:r0 + 128, c0 - NF:c0 - NF + cw], ost[:, 0:cw], pt, ost)
                k += 1
    kb.barrier([pf, pt])
    kb.release(m)


def phaseC(kb, hT_own, yT_all, xT_own, Wg, Wb, Wo, xoutT, D, T, BW=2048, scratch=None, prefilled=False):
    KC = D // P
    KBW = BW // P
    m = kb.mark()
    ht = kb.sb([P, KC, 512], BF16, "C_h")
    yt = kb.sb([P, 3 * KBW, 512], BF16, "C_y")
    mg = kb.sb([P, KC, 512], BF16, "C_m")
    wgs = [kb.sb([P, KC, P], BF16, f"C_wg{i}") for i in range(2)]
    wbs = [kb.sb([P, KBW, P], BF16, f"C_wb{i}") for i in range(2)]
    wos = [kb.sb([P, KC, P], BF16, f"C_wo{i}") for i in range(2)]
    sgs = [kb.sb([P, 512], F32, f"C_sg{i}") for i in range(2)]
    acc = kb.sb([P, 512], F32, "C_acc")
    tmp = kb.sb([P, 512], F32, "C_tmp")
    xs = [kb.sb([P, 512], F32, f"C_x{i}") for i in range(2)]
    os_ = [kb.sb([P, 512], F32, f"C_o{i}") for i in range(2)]
    psg = [kb.ps([P, 512], F32, f"C_psg{i}") for i in range(2)]
    psb = [kb.ps([P, 512], F32, f"C_psb{i}") for i in range(2)]
    pso = [kb.ps([P, 512], F32, f"C_pso{i}") for i in range(2)]
    hv = hT_own.t.rearrange("(c p) t -> p c t", p=P)
    yv = yT_all.t.rearrange("(c p) t -> p c t", p=P)
    Wgv = Wg.t.rearrange("(c p) n -> p c n", p=P)
    Wbv = Wb.t.rearrange("(c p) n -> p c n", p=P)
    Wov = Wo.t.rearrange("(c p) n -> p c n", p=P)
    u = 0; v = 0
    if scratch is not None:
        Wg16, Wb16, Wo16 = scratch
    if scratch is not None and not prefilled:
        pu = 0
        for dc in range(KC):
            for n in range(3):
                wg = wgs[pu % 2]; wb = wbs[pu % 2]; pu += 1
                c0 = n * D + dc * P
                load_kc_tile(kb, "pool", wg, lambda a, b: wg[:, a:b, :], Wg, Wgv[:, :, c0:c0 + P], KC, nsplit=2)
                load_kc_tile(kb, "pool", wb, lambda a, b: wb[:, a:b, :], Wb,
                             Wbv[:, n * KBW:(n + 1) * KBW, dc * P:(dc + 1) * P], KBW, nsplit=1)
                kb.dma("sp", Wg16[dc * 3 + n].rearrange("p (k c) -> p k c", c=P), wg[:], Wg16, wg)
                kb.dma("sp", Wb16[dc * 3 + n].rearrange("p (k c) -> p k c", c=P), wb[:], Wb16, wb)
        for ec in range(KC):
            wo = wos[pu % 2]; pu += 1
            load_kc_tile(kb, "pool", wo, lambda a, b: wo[:, a:b, :], Wo, Wov[:, :, ec * P:(ec + 1) * P], KC, nsplit=2)
            kb.dma("sp", Wo16[ec].rearrange("p (k c) -> p k c", c=P), wo[:], Wo16, wo)
    for tg in range(T // 512):
        tsl = slice(tg * 512, (tg + 1) * 512)
        load_kc_tile(kb, "sp", ht, lambda a, b: ht[:, a:b, :], hT_own, hv[:, :, tsl], KC)
        load_kc_tile(kb, "sp", yt, lambda a, b: yt[:, a:b, :], yT_all, yv[:, :, tsl], 3 * KBW)
        for dc in range(KC):
            for n in range(3):
                wg = wgs[u % 2]; wb = wbs[u % 2]; pg = psg[u % 2]; pb = psb[u % 2]; sg = sgs[u % 2]; u += 1
                c0 = n * D + dc * P
                if scratch is not None:
                    kb.dma("sp", wg[:], Wg16[dc * 3 + n].rearrange("p (k c) -> p k c", c=P), wg, Wg16)
                    kb.dma("sp", wb[:], Wb16[dc * 3 + n].rearrange("p (k c) -> p k c", c=P), wb, Wb16)
                else:
                    load_kc_tile(kb, "pool", wg, lambda a, b: wg[:, a:b, :], Wg, Wgv[:, :, c0:c0 + P], KC, nsplit=2)
                    load_kc_tile(kb, "pool", wb, lambda a, b: wb[:, a:b, :], Wb,
                                 Wbv[:, n * KBW:(n + 1) * KBW, dc * P:(dc + 1) * P], KBW, nsplit=1)
                for kc in range(KC):
                    kb.op("pe", lambda e: e.matmul(pg[:], wg[:, kc, :], ht[:, kc, :], start=(kc == 0), stop=(kc == KC - 1)),
                          reads=[wg, ht], writes=[pg], inc=(kc == KC - 1), skip_self=True)
                for kc in range(KBW):
                    kb.op("pe", lambda e: e.matmul(pb[:], wb[:, kc, :], yt[:, n * KBW + kc, :], start=(kc == 0), stop=(kc == KBW - 1)),
                          reads=[wb, yt], writes=[pb], inc=(kc == KBW - 1), skip_self=True)
                kb.op("act", lambda e: e.activation(sg[:], pg[:], AF.Sigmoid), reads=[pg], writes=[sg])
                if n == 0:
                    kb.op("dve", lambda e: e.tensor_tensor(acc[:], pb[:], sg[:], ALU.mult), reads=[pb, sg], writes=[acc])
                elif n == 1:
                    kb.op("dve", lambda e: e.tensor_tensor(tmp[:], pb[:], sg[:], ALU.mult), reads=[pb, sg], writes=[tmp])
                    kb.op("pool", lambda e: e.tensor_tensor(acc[:], acc[:], tmp[:], ALU.add), reads=[acc, tmp], writes=[acc])
                else:
                    kb.op("dve", lambda e: e.tensor_tensor(tmp[:], pb[:], sg[:], ALU.mult), reads=[pb, sg], writes=[tmp])
                    kb.op("pool", lambda e: e.tensor_tensor(mg[:, dc, :], acc[:], tmp[:], ALU.add), reads=[acc, tmp], writes=[mg])
        def load_o(ec_, vv):
            wo_ = wos[vv % 2]; xb_ = xs[vv % 2]
            if scratch is not None:
                kb.dma("sp", wo_[:], Wo16[ec_].rearrange("p (k c) -> p k c", c=P), wo_, Wo16)
            else:
                load_kc_tile(kb, "pool", wo_, lambda a, b: wo_[:, a:b, :], Wo, Wov[:, :, ec_ * P:(ec_ + 1) * P], KC, nsplit=2)
            kb.dma("sp", xb_[:], xT_own[ec_ * P:(ec_ + 1) * P, tsl], xb_, xT_own)
        load_o(0, v)
        for ec in range(KC):
            wo = wos[v % 2]; po = pso[v % 2]; xb = xs[v % 2]; ob = os_[v % 2]
            if ec + 1 < KC:
                load_o(ec + 1, v + 1)
            v += 1
            for kc in range(KC):
                kb.op("pe", lambda e: e.matmul(po[:], wo[:, kc, :], mg[:, kc, :], start=(kc == 0), stop=(kc == KC - 1)),
                      reads=[wo, mg], writes=[po], inc=(kc == KC - 1), skip_self=True)
            kb.op("dve", lambda e: e.tensor_tensor(ob[:], po[:], xb[:], ALU.add), reads=[po, xb], writes=[ob])
            kb.dma("sp", xoutT[ec * P:(ec + 1) * P, tsl], ob[:], xoutT, ob)
    kb.barrier([xoutT])
    kb.release(m)


P = 128
EPS = 1e-6
RET_Q0 = 0
RET_K0 = 512
RET_V0 = 0
RET_Z0 = 512


def y_prep(kb, C, slot, zsrc_ap, pt, gain_ap, gain_buf, W):
    zt = C["zt"][slot]; gz = C["gz"][slot]
    kb.dma("sp", zt[:, 0:W], zsrc_ap, zt, pt)
    kb.op("act", lambda e: e.activation(gz[:, 0:W], zt[:, 0:W], AF.Silu), reads=[zt], writes=[gz])
    kb.op("pool", lambda e: e.tensor_tensor(gz[:, 0:W], gz[:, 0:W], gain_ap, ALU.mult), reads=[gz, gain_buf], writes=[gz])


def y_tail(kb, C, on, slot, W, ytile_store):
    gz = C["gz"][slot]; yb = C["yb"][C["n"] % 2]
    C["n"] += 1
    kb.op("pool", lambda e: e.tensor_tensor(yb[:, 0:W], on[:, 0:W], gz[:, 0:W], ALU.mult), reads=[on, gz], writes=[yb])
    for c in range(W // P):
        pT = C["pT"][C["nt"] % 2]; C["nt"] += 1
        kb.op("pe", lambda e: e.transpose(pT[:], yb[:, c * P:(c + 1) * P], C["identb"][:]),
              reads=[yb, C["identb"]], writes=[pT])
        ytile_store(c, pT)


def tail_ctx(kb, pfx, W):
    C = dict(n=0, nt=0)
    C["zt"] = [kb.sb([P, W], F32, f"{pfx}_zt{i}") for i in range(8)]
    C["gz"] = [kb.sb([P, W], F32, f"{pfx}_gz{i}") for i in range(8)]
    C["yb"] = [kb.sb([P, W], BF16, f"{pfx}_yb{i}") for i in range(2)]
    C["pT"] = [kb.ps([P, P], BF16, f"{pfx}_pT{i}") for i in range(2)]
    identf = kb.sb([P, P], F32, f"{pfx}_identf")
    kb.op("pool", lambda e: e.memset(identf[:], 1.0), writes=[identf])
    kb.op("pool", lambda e: e.affine_select(out=identf[:], in_=identf[:], pattern=[[-1, P]], compare_op=ALU.is_equal,
                                            fill=0.0, base=0, channel_multiplier=1), reads=[identf], writes=[identf])
    identb = kb.sb([P, P], BF16, f"{pfx}_identb")
    kb.op("pool", lambda e: e.tensor_copy(identb[:], identf[:]), reads=[identf], writes=[identb])
    C["identb"] = identb
    C["identf"] = identf
    return C


def ret_mixer(kb, pf, pt, cs_d, m0_d, md_d, cf_d, gnb_d, yT, S, ybase=0):
    NB = S // P
    NG = S // 512
    m = kb.mark()
    cos = kb.sb([P, S], F32, "R_cos"); sin = kb.sb([P, S], F32, "R_sin")
    kb.dma("sp", cos[:], cs_d[0], cos, cs_d); kb.dma("sp", sin[:], cs_d[1], sin, cs_d)
    gnb = kb.sb([P, 512], F32, "R_gnb"); kb.dma("sp", gnb[:], gnb_d[:], gnb, gnb_d)
    m0 = kb.sb([P, 512], F32, "R_m0"); md = kb.sb([P, 4, 512], F32, "R_md"); cf = kb.sb([P, NB], F32, "R_cf")
    qr = kb.sb([P, 2, S], BF16, "R_qr"); kr = kb.sb([P, 2, S], BF16, "R_kr")
    v16 = kb.sb([P, NB, 256], BF16, "R_v16")
    ld = [kb.sb([P, 2, 512], F32, f"R_ld{i}") for i in range(2)]
    tms = [kb.sb([P, 512], F32, f"R_tm{i}") for i in range(4)]
    pTs = [kb.sb([P, 512], BF16, f"R_p{i}") for i in range(3)]
    sT = [kb.ps([P, 512], F32, f"R_sT{i}") for i in range(2)]
    oacc = [kb.ps([P, 256], F32, f"R_o{i}") for i in range(4)]
    st = kb.sb([P, 6], F32, "R_st"); mv = kb.sb([P, 2], F32, "R_mv"); rs = kb.sb([P, 1], F32, "R_rs")
    epst = kb.sb([P, 1], F32, "R_eps")
    kb.op("pool", lambda e: e.memset(epst[:], EPS), writes=[epst])
    ons = [kb.sb([P, 256], F32, f"R_on{i}") for i in range(2)]
    yTs = [kb.sb([P, 2, 512], BF16, f"R_yT{i}") for i in range(2)]
    C = tail_ctx(kb, "R", 256)
    nl = 0; npp = 0; ns = 0; non = 0; ny = 0
    for hh in range(2):
        kb.dma("sp", m0[:], m0_d[hh], m0, m0_d)
        kb.dma("sp", md[:], md_d[hh].rearrange("r p q -> p r q"), md, md_d)
        kb.dma("sp", cf[:], cf_d[hh], cf, cf_d)
        kb.dma("pool", v16[:], pt[:, RET_V0 + hh * 256:RET_V0 + (hh + 1) * 256].rearrange("(t p) c -> p t c", p=P), v16, pt)
        for (dst, row0) in ((qr, RET_Q0 + hh * 256), (kr, RET_K0 + hh * 256)):
            for tg in range(NG):
                tsl = slice(tg * 512, (tg + 1) * 512)
                xb = ld[nl % 2]; nl += 1
                kb.dma("sp", xb[:], pf[row0:row0 + 256, tsl].rearrange("(h p) t -> p h t", p=P), xb, pf)
                t1, t2, t3, t4 = tms
                kb.op("dve", lambda e: e.tensor_tensor(t1[:], xb[:, 0, :], cos[:, tsl], ALU.mult), reads=[xb, cos], writes=[t1])
                kb.op("pool", lambda e: e.tensor_tensor(t2[:], xb[:, 1, :], sin[:, tsl], ALU.mult), reads=[xb, sin], writes=[t2])
                kb.op("dve", lambda e: e.tensor_tensor(dst[:, 0, tsl], t1[:], t2[:], ALU.subtract), reads=[t1, t2], writes=[dst])
                kb.op("pool", lambda e: e.tensor_tensor(t3[:], xb[:, 0, :], sin[:, tsl], ALU.mult), reads=[xb, sin], writes=[t3])
                kb.op("dve", lambda e: e.tensor_tensor(t4[:], xb[:, 1, :], cos[:, tsl], ALU.mult), reads=[xb, cos], writes=[t4])
                kb.op("pool", lambda e: e.tensor_tensor(dst[:, 1, tsl], t3[:], t4[:], ALU.add), reads=[t3, t4], writes=[dst])
        items = [(G, j) for G in range(NG) for j in range(4 * G + 4)]
        pend = {}

        def emit_s(it):
            nonlocal ns
            G, j = it
            ps = sT[ns % 2]; ns += 1
            ksl = slice(j * P, (j + 1) * P); qsl_ = slice(G * 512, (G + 1) * 512)
            for hf in range(2):
                kb.op("pe", lambda e: e.matmul(ps[:], kr[:, hf, ksl], qr[:, hf, qsl_], start=(hf == 0), stop=(hf == 1)),
                      reads=[kr, qr], writes=[ps], inc=(hf == 1), skip_self=True)
            pend[it] = ps

        def emit_rest(it):
            nonlocal npp
            G, j = it
            ps = pend.pop(it)
            pb = pTs[npp % 3]; npp += 1
            r = j - 4 * G
            if r < 0:
                d = 4 * G - j - 1
                kb.op("dve", lambda e: e.scalar_tensor_tensor(pb[:], ps[:], cf[:, d:d + 1], m0[:], ALU.mult, ALU.mult),
                      reads=[ps, cf, m0], writes=[pb])
            else:
                kb.op("dve", lambda e: e.tensor_tensor(pb[:], ps[:], md[:, r, :], ALU.mult), reads=[ps, md], writes=[pb])
            for qs in range(4):
                if r > qs:
                    continue
                first = (j == 0)
                last = (j == 4 * G + qs)
                kb.op("pe", lambda e: e.matmul(oacc[qs][:], pb[:, qs * P:(qs + 1) * P], v16[:, j, :], start=first, stop=last),
                      reads=[pb, v16], writes=[oacc[qs]], skip_self=True)
        emit_s(items[0])
        for ii, it in enumerate(items):
            if ii + 1 < len(items):
                emit_s(items[ii + 1])
            G, j = it
            if j == 0:
                for qs_ in range(4):
                    tt_ = 4 * G + qs_
                    y_prep(kb, C, (G % 2) * 4 + qs_, pt[tt_ * P:(tt_ + 1) * P, RET_Z0 + hh * 256:RET_Z0 + (hh + 1) * 256], pt,
                           gnb[:, hh * 256:(hh + 1) * 256], gnb, 256)
            emit_rest(it)
            if j != 4 * G + 3:
                continue
            qsl = slice(G * 512, (G + 1) * 512)
            yts = yTs[ny % 2]; ny += 1
            for qs in range(4):
                tt = 4 * G + qs
                on = ons[non % 2]; non += 1
                o = oacc[qs]
                kb.op("dve", lambda e: e.bn_stats(st[:], o[:]), reads=[o], writes=[st])
                kb.op("dve", lambda e: e.bn_aggr(mv[:], st[:]), reads=[st], writes=[mv])
                kb.op("act", lambda e: e.activation(rs[:], mv[:, 1:2], AF.Sqrt, bias=epst[:]), reads=[mv, epst], writes=[rs])
                kb.op("dve", lambda e: e.reciprocal(rs[:], rs[:]), reads=[rs], writes=[rs])
                kb.op("dve", lambda e: e.tensor_scalar(on[:], o[:], mv[:, 0:1], rs[:], ALU.subtract, ALU.mult),
                      reads=[o, mv, rs], writes=[on])

                def store(c, pT, qs=qs, yts=yts):
                    kb.op("act", lambda e: e.copy(yts[:, c, qs * P:(qs + 1) * P], pT[:]), reads=[pT], writes=[yts])
                y_tail(kb, C, on, (G % 2) * 4 + qs, 256, store)
            for c in range(2):
                r0 = ybase + hh * 256 + c * P
                kb.dma("sp", yT[r0:r0 + P, qsl], yts[:, c, :], yT, yts)
    kb.barrier([yT])
    kb.release(m)


def ret_consts(S, heads):
    import numpy as np
    half = 128
    inv = (10000.0 ** (-np.arange(half, dtype=np.float32) / half)).astype(np.float32)
    ang = np.arange(S, dtype=np.float32)[None, :] * inv[:, None]
    cs = np.stack([np.cos(ang), np.sin(ang)]).astype(np.float32)
    NB = S // 128
    m0 = np.zeros((2, 128, 512), np.float32); md = np.zeros((2, 4, 128, 512), np.float32)
    cf = np.zeros((2, 128, NB), np.float32)
    ki = np.arange(128, dtype=np.float64)[:, None]; qi = np.arange(512, dtype=np.float64)[None, :]
    for a, h in enumerate(heads):
        lg = np.log(1.0 - 2.0 ** (-5.0 - h))
        m0[a] = np.exp(lg * (qi - ki + 128)) / 16.0
        for r in range(4):
            rel = qi - 128 * r - ki
            md[a, r] = np.where(rel >= 0, np.exp(lg * np.maximum(rel, 0)) / 16.0, 0.0)
        cf[a] = np.exp(lg * 128.0 * np.arange(NB))[None, :]
    return cs, m0, md, cf

import math


P = 128
EPS = 1e-6
DF_Q0 = 1024
DF_K0 = 1536
DF_V0 = 1024
DF_Z0 = 1536
NEG = -30000.0


def diff_mixer(kb, pf, pt, bn_d, fb_d, gqk_d, lam_d, subb_d, yT, S, layer_idx, ybase=1024):
    NB = S // P
    NG = S // 512
    lam_init = 0.8 - 0.6 * math.exp(-0.3 * layer_idx)
    m = kb.mark()
    ones = kb.sb([P, P], F32, "D_ones")
    kb.op("pool", lambda e: e.memset(ones[:], 1.0), writes=[ones])
    fb = kb.sb([P, 2], F32, "D_fb"); kb.dma("sp", fb[:], fb_d[:], fb, fb_d)
    gqk = kb.sb([P, 2], F32, "D_gqk"); kb.dma("sp", gqk[:], gqk_d[:], gqk, gqk_d)
    subb = kb.sb([P, 256], F32, "D_subb"); kb.dma("sp", subb[:], subb_d[:], subb, subb_d)
    kb.op("pool", lambda e: e.tensor_scalar(subb[:], subb[:], 1.0 - lam_init, None, ALU.mult), reads=[subb], writes=[subb])
    lv = kb.sb([P, 4, P], F32, "D_lv"); kb.dma("sp", lv[:], lam_d.t.rearrange("a p d -> p a d"), lv, lam_d)
    lt = kb.sb([P, 2, P], F32, "D_lt"); ls = kb.sb([P, 2], F32, "D_ls"); lam = kb.sb([P, 1], F32, "D_lam")
    kb.op("dve", lambda e: e.tensor_tensor(lt[:, 0, :], lv[:, 0, :], lv[:, 1, :], ALU.mult), reads=[lv], writes=[lt])
    kb.op("dve", lambda e: e.tensor_tensor(lt[:, 1, :], lv[:, 2, :], lv[:, 3, :], ALU.mult), reads=[lv, lt], writes=[lt])
    kb.op("dve", lambda e: e.reduce_sum(ls[:], lt[:], AX.X), reads=[lt], writes=[ls])
    kb.op("act", lambda e: e.activation(ls[:], ls[:], AF.Exp), reads=[ls], writes=[ls])
    kb.op("dve", lambda e: e.tensor_tensor(lam[:], ls[:, 0:1], ls[:, 1:2], ALU.subtract), reads=[ls], writes=[lam])
    kb.op("dve", lambda e: e.tensor_scalar(lam[:], lam[:], lam_init, None, ALU.add), reads=[lam], writes=[lam])
    epsq = kb.sb([P, 3], F32, "D_eps")
    kb.op("pool", lambda e: e.memset(epsq[:, 0:1], 128.0 * EPS), writes=[epsq])
    kb.op("pool", lambda e: e.memset(epsq[:, 1:2], EPS), reads=[epsq], writes=[epsq])
    bn = kb.sb([P, 5, 512], F32, "D_bn")
    qn = kb.sb([P, 2, S], BF16, "D_qn"); kn = kb.sb([P, 2, S], BF16, "D_kn")
    v16 = kb.sb([P, NB, 257], BF16, "D_v16")
    kb.op("pool", lambda e: e.memset(v16[:, :, 256:257], 1.0), writes=[v16])
    ld = [kb.sb([P, 512], F32, f"D_ld{i}") for i in range(2)]
    sq = [kb.sb([P, 512], F32, f"D_sq{i}") for i in range(2)]
    rst = [kb.sb([P, 512], F32, f"D_rst{i}") for i in range(2)]
    tb = [kb.sb([P, 512], F32, f"D_tb{i}") for i in range(2)]
    pTs = [kb.sb([P, 512], BF16, f"D_p{i}") for i in range(3)]
    sT = [kb.ps([P, 512], F32, f"D_sT{i}") for i in range(2)]
    oacc = [kb.ps([P, 257], F32, f"D_o{i}") for i in range(4)]
    om0 = kb.sb([P, 4, 257], F32, "D_om0")
    rr = kb.sb([P, 4], F32, "D_rr")
    t1 = [kb.sb([P, 256], F32, f"D_t1{i}") for i in range(2)]
    ob = [kb.sb([P, 256], F32, f"D_ob{i}") for i in range(2)]
    junk = kb.sb([P, 256], F32, "D_junk")
    ons = [kb.sb([P, 256], F32, f"D_on{i}") for i in range(2)]
    yTs = [kb.sb([P, 2, 512], BF16, f"D_yT{i}") for i in range(2)]
    C = tail_ctx(kb, "D", 256)
    nl = 0; ns = 0; npp = 0; ntb = 0; nt1 = 0; non = 0; ny = 0
    for hh in range(2):
        kb.dma("sp", bn[:], bn_d[hh].rearrange("r p q -> p r q"), bn, bn_d)
        kb.dma("pool", v16[:, :, 0:256], pt[:, DF_V0 + hh * 256:DF_V0 + (hh + 1) * 256].rearrange("(t p) c -> p t c", p=P), v16, pt)
        for (dst, row0, gi) in ((qn, DF_Q0 + hh * 256, 0), (kn, DF_K0 + hh * 256, 1)):
            for mm in range(2):
                for tg in range(NG):
                    tsl = slice(tg * 512, (tg + 1) * 512)
                    xb = ld[nl % 2]; s2 = sq[nl % 2]; rs = rst[nl % 2]; nl += 1
                    ps = sT[ns % 2]; ns += 1
                    kb.dma("sp", xb[:], pf[row0 + mm * P:row0 + (mm + 1) * P, tsl], xb, pf)
                    kb.op("act", lambda e: e.activation(s2[:], xb[:], AF.Square), reads=[xb], writes=[s2])
                    kb.op("pe", lambda e: e.matmul(ps[:], ones[:], s2[:], start=True, stop=True), reads=[ones, s2], writes=[ps], skip_self=True)
                    sc = 1.0 if gi == 0 else 1.0 / 128.0
                    kb.op("act", lambda e: e.activation(rs[:], ps[:], AF.Sqrt, bias=epsq[:, gi:gi + 1], scale=sc),
                          reads=[ps, epsq], writes=[rs])
                    kb.op("dve", lambda e: e.reciprocal(rs[:], rs[:]), reads=[rs], writes=[rs])
                    kb.op("dve", lambda e: e.scalar_tensor_tensor(dst[:, mm, tsl], xb[:], gqk[:, gi:gi + 1], rs[:], ALU.mult, ALU.mult),
                          reads=[xb, gqk, rs], writes=[dst])
        items = [(G, mm, j) for G in range(NG) for mm in range(2) for j in range(4 * G + 4)]
        pend = {}

        def emit_s(it):
            nonlocal ns
            G, mm, j = it
            ps = sT[ns % 2]; ns += 1
            ksl = slice(j * P, (j + 1) * P); qsl_ = slice(G * 512, (G + 1) * 512)
            kb.op("pe", lambda e: e.matmul(ps[:], kn[:, mm, ksl], qn[:, mm, qsl_], start=True, stop=True),
                  reads=[kn, qn], writes=[ps], skip_self=True)
            pend[it] = ps

        def emit_rest(it):
            nonlocal npp, ntb
            G, mm, j = it
            ps = pend.pop(it)
            pb = pTs[npp % 3]; npp += 1
            r = j - 4 * G
            if r <= -2:
                kb.op("act", lambda e: e.activation(pb[:], ps[:], AF.Exp, bias=fb[:, hh:hh + 1]), reads=[ps, fb], writes=[pb])
            else:
                t = tb[ntb % 2]; ntb += 1
                kb.op("dve", lambda e: e.tensor_tensor(t[:], ps[:], bn[:, r + 1, :], ALU.add), reads=[ps, bn], writes=[t])
                kb.op("act", lambda e: e.activation(pb[:], t[:], AF.Exp), reads=[t], writes=[pb])
            for qs in range(4):
                if r > qs:
                    continue
                first = (j == 0)
                last = (j == 4 * G + qs)
                kb.op("pe", lambda e: e.matmul(oacc[qs][:], pb[:, qs * P:(qs + 1) * P], v16[:, j, :], start=first, stop=last),
                      reads=[pb, v16], writes=[oacc[qs]], skip_self=True)
            if mm == 0 and j == 4 * G + 3:
                for qs in range(4):
                    if qs % 2 == 0:
                        kb.op("act", lambda e: e.copy(om0[:, qs, :], oacc[qs][:]), reads=[oacc[qs]], writes=[om0])
                    else:
                        kb.op("dve", lambda e: e.tensor_copy(om0[:, qs, :], oacc[qs][:]), reads=[oacc[qs]], writes=[om0])
        emit_s(items[0])
        for ii, it in enumerate(items):
            if ii + 1 < len(items):
                emit_s(items[ii + 1])
            G, mm, j = it
            if mm == 0 and j == 0:
                for qs_ in range(4):
                    tt_ = 4 * G + qs_
                    y_prep(kb, C, (G % 2) * 4 + qs_, pt[tt_ * P:(tt_ + 1) * P, DF_Z0 + hh * 256:DF_Z0 + (hh + 1) * 256], pt,
                           subb[:], subb, 256)
            emit_rest(it)
            if not (mm == 1 and j == 4 * G + 3):
                continue
            qsl = slice(G * 512, (G + 1) * 512)
            yts = yTs[ny % 2]; ny += 1
            for qs in range(4):
                tt = 4 * G + qs
                o1 = oacc[qs]
                tq = t1[nt1 % 2]; o = ob[nt1 % 2]; nt1 += 1
                on = ons[non % 2]; non += 1
                kb.op("dve", lambda e: e.reciprocal(rr[:, 0:1], om0[:, qs, 256:257]), reads=[om0], writes=[rr])
                kb.op("dve", lambda e: e.reciprocal(rr[:, 1:2], o1[:, 256:257]), reads=[o1, rr], writes=[rr])
                kb.op("dve", lambda e: e.tensor_tensor(rr[:, 1:2], rr[:, 1:2], lam[:], ALU.mult), reads=[rr, lam], writes=[rr])
                kb.op("dve", lambda e: e.tensor_scalar(tq[:], o1[:, 0:256], rr[:, 1:2], None, ALU.mult), reads=[o1, rr], writes=[tq])
                kb.op("dve", lambda e: e.scalar_tensor_tensor(o[:], om0[:, qs, 0:256], rr[:, 0:1], tq[:], ALU.mult, ALU.subtract),
                      reads=[om0, rr, tq], writes=[o])
                kb.op("act", lambda e: e.activation(junk[:], o[:], AF.Square, accum_out=rr[:, 2:3]), reads=[o, rr], writes=[junk, rr])
                kb.op("act", lambda e: e.activation(rr[:, 3:4], rr[:, 2:3], AF.Sqrt, bias=epsq[:, 1:2], scale=1.0 / 256.0),
                      reads=[rr, epsq], writes=[rr])
                kb.op("dve", lambda e: e.reciprocal(rr[:, 3:4], rr[:, 3:4]), reads=[rr], writes=[rr])
                kb.op("dve", lambda e: e.tensor_scalar(on[:], o[:], rr[:, 3:4], None, ALU.mult), reads=[o, rr], writes=[on])

                def store(c, pT, qs=qs, yts=yts):
                    kb.op("act", lambda e: e.copy(yts[:, c, qs * P:(qs + 1) * P], pT[:]), reads=[pT], writes=[yts])
                y_tail(kb, C, on, (G % 2) * 4 + qs, 256, store)
            for c in range(2):
                r0 = ybase + hh * 256 + c * P
                kb.dma("sp", yT[r0:r0 + P, qsl], yts[:, c, :], yT, yts)
    kb.barrier([yT])
    kb.release(m)


def rel_bucket_np(rel):
    import numpy as np
    import jax
    import jax.numpy as jnp
    with jax.default_device(jax.devices("cpu")[0]):
        return _rel_bucket_cpu(np.asarray(rel))


def _rel_bucket_cpu(rel):
    import numpy as np
    import jax.numpy as jnp
    rel = jnp.asarray(rel)
    nb = 32 // 2
    max_exact = nb // 2
    base = jnp.where(rel > 0, nb, 0)
    n = jnp.abs(rel)
    nf = jnp.maximum(n, 1).astype(jnp.float32)
    large = max_exact + (jnp.log(nf / max_exact) / math.log(128 / max_exact) * (nb - max_exact)).astype(jnp.int32)
    large = jnp.minimum(large, nb - 1)
    return np.asarray(base + jnp.where(n < max_exact, n, large))


def diff_bias_index():
    import numpy as np
    ki = np.arange(128)[:, None]; qi = np.arange(512)[None, :]
    idx = np.zeros((5, 128, 512), np.int64); vis = np.zeros((5, 128, 512), bool)
    for p in range(5):
        krel = 128 * (p - 1) + ki
        rel = krel - qi
        idx[p] = rel_bucket_np(rel)
        vis[p] = (krel // 64) <= (qi // 64)
    return idx, vis


P = 128
EPS = 1e-6
GD_Q0 = 2048
GD_K0 = 2560
GD_V0 = 3072
GD_Z0 = 2048
GD_AB0 = 2560
NEGM = -30000.0
NLEV = 6


class PsumCarver:
    def __init__(self, kb, pfx, nbanks_f32, nbanks_bf16=1):
        self.f = [kb.ps([P, 512], F32, f"{pfx}_bk{i}") for i in range(nbanks_f32)]
        self.b = [kb.ps([P, 1024], BF16, f"{pfx}_bb{i}") for i in range(nbanks_bf16)]
        for x in self.f + self.b:
            x.excl = True

    def f32(self, bank, slot, name):
        return self.f[bank].view(self.f[bank][:, slot * P:(slot + 1) * P], name)

    def bf(self, bank, slot, name):
        return self.b[bank].view(self.b[bank][:, slot * P:(slot + 1) * P], name)


def gdn_mixer(kb, pf, pt, cw_d, dtb_d, alog_d, gnb_d, gm_d, yT, S, ybase=512):
    NB = S // P
    NG = S // 512
    NC4 = NB * 4
    m = kb.mark()
    cw = kb.sb([P, 48], F32, "G_cw"); kb.dma("sp", cw[:], cw_d[:], cw, cw_d)
    gnb = kb.sb([P, P], F32, "G_gnb"); kb.dma("sp", gnb[:], gnb_d[:], gnb, gnb_d)
    gm = kb.sb([P, 4, P], F32, "G_gm"); kb.dma("sp", gm[:], gm_d.t.rearrange("a p q -> p a q"), gm, gm_d)
    UT = gm[:, 0, :]; SEL = gm[:, 1, :]; NEGL = gm[:, 2, :]; NEGU = gm[:, 3, :]
    ones = kb.sb([P, P], F32, "G_ones"); kb.op("pool", lambda e: e.memset(ones[:], 1.0), writes=[ones])
    nones = kb.sb([P, P], F32, "G_nones"); kb.op("pool", lambda e: e.memset(nones[:], -1.0), writes=[nones])
    identf = kb.sb([P, P], F32, "G_identf")
    kb.op("pool", lambda e: e.memset(identf[:], 1.0), writes=[identf])
    kb.op("pool", lambda e: e.affine_select(out=identf[:], in_=identf[:], pattern=[[-1, P]], compare_op=ALU.is_equal,
                                            fill=0.0, base=0, channel_multiplier=1), reads=[identf], writes=[identf])
    identb = kb.sb([P, P], BF16, "G_identb")
    kb.op("pool", lambda e: e.tensor_copy(identb[:], identf[:]), reads=[identf], writes=[identb])
    epst = kb.sb([P, 2], F32, "G_eps")
    kb.op("pool", lambda e: e.memset(epst[:, 0:1], EPS), writes=[epst])
    kb.op("pool", lambda e: e.memset(epst[:, 1:2], 128.0 * EPS), reads=[epst], writes=[epst])
    PS = PsumCarver(kb, "G", 7, 1)
    ab = kb.sb([P, NB, 8], F32, "G_ab")
    kb.dma("sp", ab[:], pt[:, GD_AB0:GD_AB0 + 8].rearrange("(t p) c -> p t c", p=P), ab, pt)
    dtb = kb.sb([P, NB, 4], F32, "G_dtb"); kb.dma("sp", dtb[:], dtb_d.t.rearrange("p (t h) -> p t h", h=4), dtb, dtb_d)
    alog = kb.sb([P, NB, 4], F32, "G_alog"); kb.dma("sp", alog[:], alog_d.t.rearrange("p (t h) -> p t h", h=4), alog, alog_d)
    beta = kb.sb([P, NB, 4], F32, "G_beta"); g = kb.sb([P, NB, 4], F32, "G_g"); sp_ = kb.sb([P, NB, 4], F32, "G_sp")
    gc = kb.sb([P, NB, 4], F32, "G_gc"); eg = kb.sb([P, NB, 4], F32, "G_eg"); bg = kb.sb([P, NB, 4], F32, "G_bg")
    kgs = kb.sb([P, NB, 4], F32, "G_kgs"); egl = kb.sb([P, NB, 4], F32, "G_egl")
    kb.op("act", lambda e: e.activation(beta[:], ab[:, :, 4:8], AF.Sigmoid), reads=[ab], writes=[beta])
    kb.op("dve", lambda e: e.tensor_tensor(sp_[:], ab[:, :, 0:4], dtb[:], ALU.add), reads=[ab, dtb], writes=[sp_])
    kb.op("act", lambda e: e.activation(sp_[:], sp_[:], AF.Exp), reads=[sp_], writes=[sp_])
    kb.op("act", lambda e: e.activation(sp_[:], sp_[:], AF.Ln, bias=1.0), reads=[sp_], writes=[sp_])
    kb.op("act", lambda e: e.activation(alog[:], alog[:], AF.Exp), reads=[alog], writes=[alog])
    kb.op("dve", lambda e: e.scalar_tensor_tensor(g[:], alog[:], -1.0, sp_[:], ALU.mult, ALU.mult), reads=[alog, sp_], writes=[g])
    psA = PS.f[5].view(PS.f[5][:, 0:NC4], "G_psA"); psB = PS.f[5].view(PS.f[5][:, 256:256 + NC4], "G_psB")
    gflat = g[:].rearrange("p t h -> p (t h)")
    kb.op("pe", lambda e: e.matmul(psA[:], UT, gflat, start=True, stop=True), reads=[gm, g], writes=[psA])
    kb.op("dve", lambda e: e.tensor_copy(gc[:].rearrange("p t h -> p (t h)"), psA[:]), reads=[psA], writes=[gc])
    kb.op("pe", lambda e: e.matmul(psB[:], SEL, gc[:].rearrange("p t h -> p (t h)"), start=True, stop=True), reads=[gm, gc], writes=[psB])
    kb.op("act", lambda e: e.activation(eg[:], gc[:], AF.Exp), reads=[gc], writes=[eg])
    kb.op("dve", lambda e: e.tensor_tensor(bg[:], beta[:], eg[:], ALU.mult), reads=[beta, eg], writes=[bg])
    kb.op("act", lambda e: e.activation(egl[:].rearrange("p t h -> p (t h)"), psB[:], AF.Exp), reads=[psB], writes=[egl])
    kb.op("dve", lambda e: e.tensor_tensor(kgs[:].rearrange("p t h -> p (t h)"), psB[:], gc[:].rearrange("p t h -> p (t h)"), ALU.subtract),
          reads=[psB, gc], writes=[kgs])
    kb.op("act", lambda e: e.activation(kgs[:], kgs[:], AF.Exp), reads=[kgs], writes=[kgs])
    xin = [kb.sb([P, 515], F32, f"G_xin{i}") for i in range(3)]
    cacc = [kb.sb([P, 512], F32, f"G_cacc{i}") for i in range(2)]
    csil = [kb.sb([P, 512], F32, f"G_csil{i}") for i in range(2)]
    csq = [kb.sb([P, 512], F32, f"G_csq{i}") for i in range(2)]
    crs = [kb.sb([P, 512], F32, f"G_crs{i}") for i in range(2)]
    qT = [[kb.sb([P, 512], BF16, f"G_qT{h}_{i}") for i in range(2)] for h in range(4)]
    kT = [[kb.sb([P, 512], BF16, f"G_kT{h}_{i}") for i in range(2)] for h in range(4)]
    vT = [[kb.sb([P, 512], BF16, f"G_vT{h}_{i}") for i in range(2)] for h in range(4)]
    S32 = [kb.sb([P, P], F32, f"G_S32_{h}") for h in range(4)]
    S16 = [kb.sb([P, P], BF16, f"G_S16_{h}") for h in range(4)]
    for h in range(4):
        kb.op("pool", lambda e: e.memset(S32[h][:], 0.0), writes=[S32[h]])
        kb.op("pool", lambda e: e.memset(S16[h][:], 0.0), writes=[S16[h]])

    def four(name, dt):
        return [kb.sb([P, P], dt, f"G_{name}{i}") for i in range(4)]
    GKb = four("GKb", BF16); Kg = four("Kg", BF16); Vb = four("Vb", BF16); gL = four("gL", F32)
    t1 = four("t1", F32); e1 = four("e1", F32); t2 = four("t2", F32); e2 = four("e2", F32)
    Acur = [four("Aa", F32), four("Ab", F32)]; Bcur = [four("Ba", F32), four("Bb", F32)]
    X = four("X", F32); X16 = four("X16", BF16); AttnT = four("AttnT", BF16)
    U = four("U", F32); WT = four("WT", BF16); Vn = four("Vn", BF16); qss = four("qss", F32); O = four("O", F32)
    junk = four("junk", F32); on = four("on", F32)
    rr = [kb.sb([P, 2], F32, f"G_rr{i}") for i in range(4)]
    zt = four("zt", F32); gz = four("gz", F32); yb = four("yb", BF16)
    yts = [[kb.sb([P, 512], BF16, f"G_yt{h}_{i}") for i in range(2)] for h in range(4)]
    SSB = PS.f[6]
    state = dict(nb=0)

    def pst(name):
        bank = state["nb"] % 6
        state["nb"] += 1
        return [PS.f32(bank, h, f"G_p{name}{h}") for h in range(4)]

    def each(fn):
        for h in range(4):
            fn(h)
    nx = 0; ncv = 0
    for tg in range(NG):
        par = tg % 2
        for h in range(4):
            for ti, (row0, dstl) in enumerate(((GD_Q0, qT), (GD_K0, kT), (GD_V0, vT))):
                xb = xin[nx % 3]; nx += 1
                r0 = row0 + h * P
                if tg == 0:
                    kb.op("pool", lambda e: e.memset(xb[:, 0:3], 0.0), writes=[xb])
                    kb.dma("sp", xb[:, 3:515], pf[r0:r0 + P, 0:512], xb, pf)
                else:
                    kb.dma("sp", xb[:], pf[r0:r0 + P, tg * 512 - 3:tg * 512 + 512], xb, pf)
                ca = cacc[ncv % 2]; cs = csil[ncv % 2]; s2 = csq[ncv % 2]; rs = crs[ncv % 2]; ncv += 1
                wb = ti * 16 + h * 4
                kb.op("dve", lambda e: e.tensor_scalar(ca[:], xb[:, 3:515], cw[:, wb + 3:wb + 4], None, ALU.mult), reads=[xb, cw], writes=[ca])
                for j in (2, 1, 0):
                    kb.op("dve", lambda e: e.scalar_tensor_tensor(ca[:], xb[:, j:j + 512], cw[:, wb + j:wb + j + 1], ca[:], ALU.mult, ALU.add),
                          reads=[xb, cw, ca], writes=[ca])
                dst = dstl[h][par]
                if ti == 2:
                    kb.op("act", lambda e: e.activation(dst[:], ca[:], AF.Silu), reads=[ca], writes=[dst])
                else:
                    kb.op("act", lambda e: e.activation(cs[:], ca[:], AF.Silu), reads=[ca], writes=[cs])
                    kb.op("act", lambda e: e.activation(s2[:], cs[:], AF.Square), reads=[cs], writes=[s2])
                    kb.op("pe", lambda e: e.matmul(SSB[:], ones[:], s2[:], start=True, stop=True), reads=[ones, s2], writes=[SSB], skip_self=True)
                    if ti == 0:
                        kb.op("act", lambda e: e.activation(rs[:], SSB[:], AF.Sqrt, bias=epst[:, 1:2], scale=128.0), reads=[SSB, epst], writes=[rs])
                    else:
                        kb.op("act", lambda e: e.activation(rs[:], SSB[:], AF.Sqrt, bias=epst[:, 0:1]), reads=[SSB, epst], writes=[rs])
                    kb.op("dve", lambda e: e.reciprocal(rs[:], rs[:]), reads=[rs], writes=[rs])
                    kb.op("dve", lambda e: e.tensor_tensor(dst[:], cs[:], rs[:], ALU.mult), reads=[cs, rs], writes=[dst])
        for c in range(4):
            tt = tg * 4 + c
            sl = slice(c * P, (c + 1) * P)
            sc = lambda t, h: t[:, tt, h:h + 1]
            kTh = [kT[h][par] for h in range(4)]; qTh = [qT[h][par] for h in range(4)]; vTh = [vT[h][par] for h in range(4)]
            each(lambda h: kb.dma("sp", zt[h][:], pt[tt * P:(tt + 1) * P, GD_Z0 + h * P:GD_Z0 + (h + 1) * P], zt[h], pt))
            each(lambda h: kb.op("act", lambda e: e.activation(gz[h][:], zt[h][:], AF.Silu), reads=[zt[h]], writes=[gz[h]]))
            each(lambda h: kb.op("pool", lambda e: e.tensor_tensor(gz[h][:], gz[h][:], gnb[:], ALU.mult), reads=[gz[h], gnb], writes=[gz[h]]))
            p_trK = [PS.bf(0, h, f"G_ptrK{h}") for h in range(4)]
            p_trV = [PS.bf(0, 4 + h, f"G_ptrV{h}") for h in range(4)]
            each(lambda h: kb.op("pe", lambda e: e.transpose(p_trK[h][:], kTh[h][:, sl], identb[:]), reads=[kTh[h], identb], writes=[p_trK[h]], skip_self=True))
            each(lambda h: kb.op("act", lambda e: e.activation(GKb[h][:], p_trK[h][:], AF.Copy, scale=sc(bg, h)), reads=[p_trK[h], bg], writes=[GKb[h]]))
            each(lambda h: kb.op("dve", lambda e: e.tensor_scalar(Kg[h][:], p_trK[h][:], sc(kgs, h), None, ALU.mult), reads=[p_trK[h], kgs], writes=[Kg[h]]))
            each(lambda h: kb.op("pe", lambda e: e.transpose(p_trV[h][:], vTh[h][:, sl], identb[:]), reads=[vTh[h], identb], writes=[p_trV[h]], skip_self=True))
            each(lambda h: kb.op("act", lambda e: e.activation(Vb[h][:], p_trV[h][:], AF.Copy, scale=sc(beta, h)), reads=[p_trV[h], beta], writes=[Vb[h]]))
            p_G = pst("G"); p_PT = pst("PT"); p_D = pst("D")
            each(lambda h: kb.op("pe", lambda e: e.matmul(p_G[h][:], kTh[h][:, sl], kTh[h][:, sl], start=True, stop=True), reads=[kTh[h]], writes=[p_G[h]], skip_self=True))
            each(lambda h: kb.op("pe", lambda e: e.matmul(p_PT[h][:], kTh[h][:, sl], qTh[h][:, sl], start=True, stop=True), reads=[kTh[h], qTh[h]], writes=[p_PT[h]], skip_self=True))
            each(lambda h: kb.op("dve", lambda e: e.tensor_scalar(gL[h][:], UT, sc(g, h), None, ALU.mult), reads=[gm, g], writes=[gL[h]]))

            def st_D(h):
                kb.op("pe", lambda e: e.matmul(p_D[h][:], gL[h][:], ones[:], start=True, stop=False), reads=[gL[h], ones], writes=[p_D[h]], inc=False, skip_self=True)
                kb.op("pe", lambda e: e.matmul(p_D[h][:], nones[:], gL[h][:], start=False, stop=True), reads=[gL[h], nones], writes=[p_D[h]], skip_self=True)
            each(st_D)
            each(lambda h: kb.op("dve", lambda e: e.tensor_tensor(t1[h][:], p_D[h][:], NEGL, ALU.add), reads=[p_D[h], gm], writes=[t1[h]]))
            each(lambda h: kb.op("act", lambda e: e.activation(e1[h][:], t1[h][:], AF.Exp), reads=[t1[h]], writes=[e1[h]]))
            each(lambda h: kb.op("dve", lambda e: e.scalar_tensor_tensor(t2[h][:], p_D[h][:], -1.0, NEGU, ALU.mult, ALU.add), reads=[p_D[h], gm], writes=[t2[h]]))
            each(lambda h: kb.op("act", lambda e: e.activation(e2[h][:], t2[h][:], AF.Exp), reads=[t2[h]], writes=[e2[h]]))
            A0 = Acur[0]; B0 = Bcur[0]
            each(lambda h: kb.op("dve", lambda e: e.scalar_tensor_tensor(A0[h][:], p_G[h][:], sc(beta, h), e1[h][:], ALU.mult, ALU.mult), reads=[p_G[h], beta, e1[h]], writes=[A0[h]]))
            each(lambda h: kb.op("dve", lambda e: e.tensor_tensor(AttnT[h][:], p_PT[h][:], e2[h][:], ALU.mult), reads=[p_PT[h], e2[h]], writes=[AttnT[h]]))
            p_Bt = pst("Bt")
            each(lambda h: kb.op("pe", lambda e: e.transpose(p_Bt[h][:], A0[h][:], identf[:]), reads=[A0[h], identf], writes=[p_Bt[h]], skip_self=True))
            each(lambda h: kb.op("act", lambda e: e.copy(B0[h][:], p_Bt[h][:]), reads=[p_Bt[h]], writes=[B0[h]]))
            each(lambda h: kb.op("dve", lambda e: e.tensor_tensor(X[h][:], identf[:], p_Bt[h][:], ALU.subtract), reads=[identf, p_Bt[h]], writes=[X[h]]))
            for lev in range(NLEV):
                Ac = Acur[lev % 2]; Bc = Bcur[lev % 2]
                An = Acur[(lev + 1) % 2]; Bn = Bcur[(lev + 1) % 2]
                last = (lev == NLEV - 1)
                p_A2 = pst("A2")
                each(lambda h: kb.op("pe", lambda e: e.matmul(p_A2[h][:], Bc[h][:], Ac[h][:], start=True, stop=True), reads=[Ac[h], Bc[h]], writes=[p_A2[h]], skip_self=True))
                if not last:
                    p_B2 = pst("B2")
                    each(lambda h: kb.op("pe", lambda e: e.matmul(p_B2[h][:], Ac[h][:], Bc[h][:], start=True, stop=True), reads=[Ac[h], Bc[h]], writes=[p_B2[h]], skip_self=True))
                each(lambda h: kb.op("act", lambda e: e.copy(An[h][:], p_A2[h][:]), reads=[p_A2[h]], writes=[An[h]]))
                if not last:
                    each(lambda h: kb.op("dve", lambda e: e.tensor_copy(Bn[h][:], p_B2[h][:]), reads=[p_B2[h]], writes=[Bn[h]]))
                p_Xn = pst("Xn")
                each(lambda h: kb.op("pe", lambda e: e.matmul(p_Xn[h][:], An[h][:], X[h][:], start=True, stop=True), reads=[An[h], X[h]], writes=[p_Xn[h]], skip_self=True))
                if not last:
                    each(lambda h: kb.op("dve", lambda e: e.tensor_tensor(X[h][:], X[h][:], p_Xn[h][:], ALU.add), reads=[X[h], p_Xn[h]], writes=[X[h]]))
                else:
                    each(lambda h: kb.op("dve", lambda e: e.tensor_tensor(X16[h][:], X[h][:], p_Xn[h][:], ALU.add), reads=[X[h], p_Xn[h]], writes=[X16[h]]))
            p_U = pst("U"); p_WT = pst("WT")
            each(lambda h: kb.op("pe", lambda e: e.matmul(p_U[h][:], X16[h][:], Vb[h][:], start=True, stop=True), reads=[X16[h], Vb[h]], writes=[p_U[h]], skip_self=True))
            each(lambda h: kb.op("pe", lambda e: e.matmul(p_WT[h][:], GKb[h][:], X16[h][:], start=True, stop=True), reads=[GKb[h], X16[h]], writes=[p_WT[h]], skip_self=True))
            each(lambda h: kb.op("act", lambda e: e.copy(U[h][:], p_U[h][:]), reads=[p_U[h]], writes=[U[h]]))
            each(lambda h: kb.op("act", lambda e: e.copy(WT[h][:], p_WT[h][:]), reads=[p_WT[h]], writes=[WT[h]]))
            p_WS = pst("WS"); p_QS = pst("QS")
            each(lambda h: kb.op("pe", lambda e: e.matmul(p_WS[h][:], WT[h][:], S16[h][:], start=True, stop=True), reads=[WT[h], S16[h]], writes=[p_WS[h]], skip_self=True))
            each(lambda h: kb.op("pe", lambda e: e.matmul(p_QS[h][:], qTh[h][:, sl], S16[h][:], start=True, stop=True), reads=[qTh[h], S16[h]], writes=[p_QS[h]], skip_self=True))
            each(lambda h: kb.op("dve", lambda e: e.tensor_tensor(Vn[h][:], U[h][:], p_WS[h][:], ALU.subtract), reads=[U[h], p_WS[h]], writes=[Vn[h]]))
            each(lambda h: kb.op("act", lambda e: e.activation(qss[h][:], p_QS[h][:], AF.Copy, scale=sc(eg, h)), reads=[p_QS[h], eg], writes=[qss[h]]))
            p_AV = pst("AV"); p_KV = pst("KV")
            each(lambda h: kb.op("pe", lambda e: e.matmul(p_AV[h][:], AttnT[h][:], Vn[h][:], start=True, stop=True), reads=[AttnT[h], Vn[h]], writes=[p_AV[h]], skip_self=True))
            each(lambda h: kb.op("pe", lambda e: e.matmul(p_KV[h][:], Kg[h][:], Vn[h][:], start=True, stop=True), reads=[Kg[h], Vn[h]], writes=[p_KV[h]], skip_self=True))
            each(lambda h: kb.op("dve", lambda e: e.tensor_tensor(O[h][:], p_AV[h][:], qss[h][:], ALU.add), reads=[p_AV[h], qss[h]], writes=[O[h]]))
            each(lambda h: kb.op("dve", lambda e: e.scalar_tensor_tensor(S32[h][:], S32[h][:], sc(egl, h), p_KV[h][:], ALU.mult, ALU.add),
                                 reads=[S32[h], egl, p_KV[h]], writes=[S32[h]]))
            each(lambda h: kb.op("act", lambda e: e.copy(S16[h][:], S32[h][:]), reads=[S32[h]], writes=[S16[h]]))
            each(lambda h: kb.op("act", lambda e: e.activation(junk[h][:], O[h][:], AF.Square, accum_out=rr[h][:, 0:1]), reads=[O[h], rr[h]], writes=[junk[h], rr[h]]))
            each(lambda h: kb.op("act", lambda e: e.activation(rr[h][:, 1:2], rr[h][:, 0:1], AF.Sqrt, bias=epst[:, 0:1], scale=1.0 / 128.0), reads=[rr[h], epst], writes=[rr[h]]))
            each(lambda h: kb.op("dve", lambda e: e.reciprocal(rr[h][:, 1:2], rr[h][:, 1:2]), reads=[rr[h]], writes=[rr[h]]))
            each(lambda h: kb.op("dve", lambda e: e.tensor_scalar(on[h][:], O[h][:], rr[h][:, 1:2], None, ALU.mult), reads=[O[h], rr[h]], writes=[on[h]]))
            each(lambda h: kb.op("pool", lambda e: e.tensor_tensor(yb[h][:], on[h][:], gz[h][:], ALU.mult), reads=[on[h], gz[h]], writes=[yb[h]]))
            p_tl = [PS.bf(0, h, f"G_ptl{h}") for h in range(4)]
            each(lambda h: kb.op("pe", lambda e: e.transpose(p_tl[h][:], yb[h][:], identb[:]), reads=[yb[h], identb], writes=[p_tl[h]], skip_self=True))
            each(lambda h: kb.op("act", lambda e: e.copy(yts[h][par][:, sl], p_tl[h][:]), reads=[p_tl[h]], writes=[yts[h][par]]))
        for h in range(4):
            r0 = ybase + h * P
            kb.dma("sp", yT[r0:r0 + P, tg * 512:(tg + 1) * 512], yts[h][par][:], yT, yts[h][par])
    kb.barrier([yT])
    kb.release(m)


def gdn_masks():
    import numpy as np
    t = np.arange(128)
    UT = (t[:, None] <= t[None, :]).astype(np.float32)
    SEL = np.zeros((128, 128), np.float32); SEL[127, :] = 1.0
    NEGL = np.where(t[:, None] > t[None, :], 0.0, NEGM).astype(np.float32)
    NEGU = np.where(t[None, :] >= t[:, None], 0.0, NEGM).astype(np.float32)
    return np.stack([UT, SEL, NEGL, NEGU])


import numpy as _np
from concourse.bass_utils import run_bass_kernel_spmd

D_MODEL = 4096
SEQ = 4096
BATCH = 2
DEPTH = 2
NF = 3584
NT = 2568
TQ = 1024
NCORES = 8
_OFF = {}
_sizes = (2048, 2048, 2048, 2048, 2048, 2048, 2048, 2048, 16, 16, 2048, 2048, 2048, 2048, 3 * 4096)
_names = ("rq", "rk", "rv", "rz", "gq", "gk", "gv", "gz", "ga", "gb", "dq", "dk", "dv", "dz", "gate")
_o = 0
for _n, _s in zip(_names, _sizes):
    _OFF[_n] = _o
    _o += _s


def my_cols(g):
    r = lambda name, w=512: _np.arange(_OFF[name] + g * w, _OFF[name] + (g + 1) * w)
    feat = [r("rq"), r("rk"), r("dq"), r("dk"), r("gq"), r("gk"), r("gv")]
    tok = [r("rv"), r("rz"), r("dv"), r("dz"), r("gz"), r("ga", 4), r("gb", 4)]
    return _np.concatenate(feat + tok)


def build_L1():
    nc = bass.Bass("TRN2", target_bir_lowering=False)
    kb = KB(nc)
    xT = kb.dram("xT", [D_MODEL, TQ], F32, kind="ExternalInput")
    g = kb.dram("gcol", [128, D_MODEL // 128], F32, kind="ExternalInput")
    hT = kb.dram("hT", [D_MODEL, TQ], BF16, kind="ExternalOutput")
    gs = kb.sb([128, D_MODEL // 128], F32, "gain")
    kb.dma("sp", gs[:], g[:], gs, g)
    phaseA(kb, xT, gs, hT, D_MODEL, TQ)
    kb.finish([hT]); kb.close()
    return nc


L2_INPUTS = dict(cs=[2, 128, SEQ], m0=[2, 128, 512], md=[2, 4, 128, 512], cf=[2, 128, SEQ // 128], gnbr=[128, 512],
                 bn=[2, 5, 128, 512], fb=[128, 2], gqk=[128, 2], lamv=[4, 128, 128], subb=[128, 256],
                 cw=[128, 48], dtb=[128, SEQ // 128 * 4], alog=[128, SEQ // 128 * 4], gnbg=[128, 128], gm=[4, 128, 128])


def emit_L2(kb, hT_parts, W, C, yT, layer_idx, S=SEQ, D=D_MODEL):
    pf = kb.dram(f"pf{layer_idx}", [NF, S], F32)
    pt = kb.dram(f"pt{layer_idx}", [S, NT], F32)
    phaseB1(kb, hT_parts, TQ, W, pf, pt, D, S, NF, NT)
    ret_mixer(kb, pf, pt, C["cs"], C["m0"], C["md"], C["cf"], C["gnbr"], yT, S, ybase=0)
    gdn_mixer(kb, pf, pt, C["cw"], C["dtb"], C["alog"], C["gnbg"], C["gm"], yT, S, ybase=512)
    diff_mixer(kb, pf, pt, C["bn"], C["fb"], C["gqk"], C["lamv"], C["subb"], yT, S, layer_idx, ybase=1024)


def build_L2(layer_idx):
    nc = bass.Bass("TRN2", target_bir_lowering=False)
    kb = KB(nc)
    hT_all = kb.dram("hT_all", [4 * D_MODEL, TQ], BF16, kind="ExternalInput")
    parts = [hT_all.view(hT_all[r * D_MODEL:(r + 1) * D_MODEL, :], f"hTp{r}") for r in range(4)]
    W = kb.dram("W", [D_MODEL, NF + NT], F32, kind="ExternalInput")
    C = {k: kb.dram(k, shp, F32, kind="ExternalInput") for k, shp in L2_INPUTS.items()}
    yT = kb.dram("yT", [1536, SEQ], BF16, kind="ExternalOutput")
    emit_L2(kb, parts, W, C, yT, layer_idx)
    kb.finish([yT]); kb.close()
    return nc


def build_L3():
    nc = bass.Bass("TRN2", target_bir_lowering=False)
    kb = KB(nc)
    hT = kb.dram("hT", [D_MODEL, TQ], BF16, kind="ExternalInput")
    yT = kb.dram("yT_all", [3 * 2048, TQ], BF16, kind="ExternalInput")
    xT = kb.dram("xT", [D_MODEL, TQ], F32, kind="ExternalInput")
    Wg = kb.dram("Wg", [D_MODEL, 3 * D_MODEL], F32, kind="ExternalInput")
    Wb = kb.dram("Wb", [3 * 2048, D_MODEL], F32, kind="ExternalInput")
    Wo = kb.dram("Wo", [D_MODEL, D_MODEL], F32, kind="ExternalInput")
    xo = kb.dram("xo", [D_MODEL, TQ], F32, kind="ExternalOutput")
    phaseC(kb, hT, yT, xT, Wg, Wb, Wo, xo, D_MODEL, TQ)
    kb.finish([xo]); kb.close()
    return nc


_DIFF_IDX = None


def layer_consts(inp, l, g):
    global _DIFF_IDX
    S = SEQ
    NB = S // 128
    rep = lambda v, n=128: _np.ascontiguousarray(_np.broadcast_to(_np.asarray(v, _np.float32).reshape(1, -1), (n, _np.asarray(v).size)))
    c = {}
    cs, m0, md, cf = ret_consts(S, [2 * g, 2 * g + 1])
    c.update(cs=cs, m0=m0, md=md, cf=cf)
    c["gnbr"] = rep(inp["ret_gn_gain"][l][g * 512:(g + 1) * 512])
    if _DIFF_IDX is None:
        _DIFF_IDX = diff_bias_index()
    idx, vis = _DIFF_IDX
    rb = _np.asarray(inp["rel_bias"], _np.float32)
    bn = _np.empty((2, 5, 128, 512), _np.float32)
    for a in range(2):
        bn[a] = _np.where(vis, rb[idx, 2 * g + a], _np.float32(NEG))
    c["bn"] = bn
    c["fb"] = rep(rb[15, 2 * g:2 * g + 2])
    c["gqk"] = _np.ascontiguousarray(_np.stack([inp["diff_q_gain"][l], inp["diff_k_gain"][l]], 1).astype(_np.float32))
    c["lamv"] = _np.stack([rep(inp[k][l]) for k in ("diff_lambda_q1", "diff_lambda_k1", "diff_lambda_q2", "diff_lambda_k2")])
    c["subb"] = rep(inp["diff_subln_gain"][l])
    cwl = _np.asarray(inp["gdn_conv_w"][l], _np.float32)
    cw = _np.empty((128, 3, 4, 4), _np.float32)
    for t in range(3):
        for i in range(4):
            h = 4 * g + i
            cw[:, t, i, :] = cwl[:, t * 2048 + h * 128:t * 2048 + (h + 1) * 128].T
    c["cw"] = cw.reshape(128, 48)
    c["dtb"] = rep(_np.tile(_np.asarray(inp["gdn_dt_bias"][l], _np.float32)[4 * g:4 * g + 4], NB))
    c["alog"] = rep(_np.tile(_np.asarray(inp["gdn_a_log"][l], _np.float32)[4 * g:4 * g + 4], NB))
    c["gnbg"] = rep(inp["gdn_norm_gain"][l])
    c["gm"] = gdn_masks()
    return c


_PROGS = {}


def _prog(key, fn):
    if key not in _PROGS:
        _PROGS[key] = fn()
    return _PROGS[key]


def kernel(**inp):
    inp = {k: _np.asarray(v) for k, v in inp.items()}
    x = inp["x"].astype(_np.float32)
    cores = list(range(NCORES))
    xT = [_np.ascontiguousarray(x[c // 4, (c % 4) * TQ:((c % 4) + 1) * TQ, :].T) for c in cores]
    for l in range(DEPTH):
        gcol = _np.ascontiguousarray(inp["norm_gain"][l].astype(_np.float32).reshape(D_MODEL // 128, 128).T)
        r1 = run_bass_kernel_spmd(_prog("L1", build_L1), [{"xT": xT[c], "gcol": gcol} for c in cores], core_ids=cores).results
        hT = [r1[c]["hT"] for c in cores]
        w_in = inp["w_in"][l]
        in2 = []
        for c in cores:
            b, g = c // 4, c % 4
            m = {"hT_all": _np.concatenate([hT[b * 4 + r] for r in range(4)], axis=0),
                 "W": _np.ascontiguousarray(w_in[:, my_cols(g)], dtype=_np.float32)}
            m.update(layer_consts(inp, l, g))
            in2.append(m)
        r2 = run_bass_kernel_spmd(_prog(("L2", l), lambda: build_L2(l)), in2, core_ids=cores).results
        del in2
        yT = [r2[c]["yT"] for c in cores]
        Wg = _np.ascontiguousarray(w_in[:, _OFF["gate"]:], dtype=_np.float32)
        Wb = _np.ascontiguousarray(inp["w_branch"][l].reshape(3 * 2048, D_MODEL), dtype=_np.float32)
        Wo = _np.ascontiguousarray(inp["w_out"][l], dtype=_np.float32)
        in3 = []
        for c in cores:
            b, g = c // 4, c % 4
            ya = _np.empty((3, 4, 512, TQ), yT[0].dtype)
            for n in range(3):
                for r in range(4):
                    ya[n, r] = yT[b * 4 + r][n * 512:(n + 1) * 512, g * TQ:(g + 1) * TQ]
            in3.append({"hT": hT[c], "yT_all": ya.reshape(3 * 2048, TQ), "xT": xT[c], "Wg": Wg, "Wb": Wb, "Wo": Wo})
        r3 = run_bass_kernel_spmd(_prog("L3", build_L3), in3, core_ids=cores).results
        del in3
        xT = [r3[c]["xo"] for c in cores]
    out = _np.empty((BATCH, SEQ, D_MODEL), _np.float32)
    for c in cores:
        out[c // 4, (c % 4) * TQ:((c % 4) + 1) * TQ, :] = xT[c].T
    return out


NCOLS = NF + NT
FUSED_CONSTS = dict(cs=[2, 128, SEQ], m0=[4, 2, 128, 512], md=[4, 2, 4, 128, 512], cf=[4, 2, 128, SEQ // 128],
                    gnbr=[DEPTH, 4, 128, 512], bn=[4, 2, 5, 128, 512], fb=[4, 128, 2], gqk=[DEPTH, 128, 2],
                    lamv=[DEPTH, 4, 128, 128], subb=[DEPTH, 128, 256], cw=[DEPTH, 4, 128, 48],
                    dtb=[DEPTH, 4, 128, SEQ // 128 * 4], alog=[DEPTH, 4, 128, SEQ // 128 * 4], gnbg=[DEPTH, 128, 128],
                    gm=[4, 128, 128])
PER_LAYER = ("gnbr", "gqk", "lamv", "subb", "cw", "dtb", "alog", "gnbg")
PER_GROUP = ("m0", "md", "cf", "gnbr", "bn", "fb", "cw", "dtb", "alog")


DBG = dict(layers=DEPTH, groups=(0, 1, 2, 3), mixers=('r', 'g', 'd'), A=True, B=True, C=True)


def build_fused():
    nc = bass.Bass("TRN2", target_bir_lowering=False)
    kb = KB(nc)
    D, S = D_MODEL, SEQ
    xT = kb.dram("xT", [D, S], F32, kind="ExternalInput")
    gcol = kb.dram("gcol", [128, DEPTH * (D // 128)], F32, kind="ExternalInput")
    W = [kb.dram(f"W{l}", [D, 4 * NCOLS], F32, kind="ExternalInput") for l in range(DEPTH)]
    Wg = [kb.dram(f"Wg{l}", [D, 3 * D], F32, kind="ExternalInput") for l in range(DEPTH)]
    Wb = [kb.dram(f"Wb{l}", [3 * 2048, D], F32, kind="ExternalInput") for l in range(DEPTH)]
    Wo = [kb.dram(f"Wo{l}", [D, D], F32, kind="ExternalInput") for l in range(DEPTH)]
    C = {k: kb.dram(k, shp, F32, kind="ExternalInput") for k, shp in FUSED_CONSTS.items()}
    xo = kb.dram("xo", [D, S], F32, kind="ExternalOutput")
    hT = kb.dram("hT_s", [D, S], BF16)
    pf = kb.dram("pf_s", [NF, S], F32)
    pt = kb.dram("pt_s", [S, NT], F32)
    yT = kb.dram("yT_s", [3 * 2048, S], BF16)
    x1 = kb.dram("x1_s", [D, S], F32)
    wscr = (kb.dram("wg16_s", [3 * (D // 128), 128, D], BF16), kb.dram("wb16_s", [3 * (D // 128), 128, 2048], BF16),
            kb.dram("wo16_s", [D // 128, 128, D], BF16))
    gs = kb.sb([128, DEPTH * (D // 128)], F32, "gain")
    kb.dma("sp", gs[:], gcol[:], gs, gcol)
    for l in range(DBG['layers']):
        xin = xT if l == 0 else x1
        xout = x1 if l < DBG['layers'] - 1 else xo
        gl = gs.view(gs[:, l * (D // 128):(l + 1) * (D // 128)], f"gain{l}")
        pm = kb.mark()
        stg = ([kb.sb([128, D // 128, 128], BF16, f"pre_wg{l}_{i}") for i in range(2)],
               [kb.sb([128, 16, 128], BF16, f"pre_wb{l}_{i}") for i in range(2)])
        for _b in stg[0] + stg[1]:
            _b.persist = True
        pump = wcast_units(kb, Wg[l], Wb[l], Wo[l], wscr, stg, D) if (DBG['B'] and DBG['C']) else None
        if DBG['A']:
            phaseA(kb, xin, gl, hT, D, S)
        for gg in DBG['groups']:
            Cg = {}
            for k, buf in C.items():
                ap = buf.t
                if k in PER_LAYER:
                    ap = ap[l]
                if k in PER_GROUP:
                    ap = ap[gg]
                Cg[k] = buf.view(ap, f"{k}_{l}_{gg}")
            if DBG['B']:
                phaseB1(kb, [hT], S, W[l], pf, pt, D, S, NF, NT, wcol0=gg * NCOLS, pump=pump)
            if 'r' in DBG['mixers']:
              ret_mixer(kb, pf, pt, Cg["cs"], Cg["m0"], Cg["md"], Cg["cf"], Cg["gnbr"], yT, S, ybase=0 * 2048 + gg * 512)
            if 'g' in DBG['mixers']:
              gdn_mixer(kb, pf, pt, Cg["cw"], Cg["dtb"], Cg["alog"], Cg["gnbg"], Cg["gm"], yT, S, ybase=1 * 2048 + gg * 512)
            if 'd' in DBG['mixers']:
              diff_mixer(kb, pf, pt, Cg["bn"], Cg["fb"], Cg["gqk"], Cg["lamv"], Cg["subb"], yT, S, l, ybase=2 * 2048 + gg * 512)
        if pump is not None:
            for _ in pump:
                pass
        kb.barrier()
        kb.release(pm)
        if DBG['C']:
            phaseC(kb, hT, yT, xin, Wg[l], Wb[l], Wo[l], xout, D, S, scratch=wscr, prefilled=(pump is not None))
    kb.finish([xo]); kb.close()
    return nc


def kernel_fused(inp):
    x = inp["x"].astype(_np.float32)
    cores = list(range(NCORES))
    shared = {}
    shared["gcol"] = _np.ascontiguousarray(_np.concatenate(
        [inp["norm_gain"][l].astype(_np.float32).reshape(D_MODEL // 128, 128).T for l in range(DEPTH)], axis=1))
    allcols = _np.concatenate([my_cols(g) for g in range(4)])
    for l in range(DEPTH):
        w_in = inp["w_in"][l]
        shared[f"W{l}"] = _np.ascontiguousarray(w_in[:, allcols], dtype=_np.float32)
        shared[f"Wg{l}"] = _np.ascontiguousarray(w_in[:, _OFF["gate"]:], dtype=_np.float32)
        shared[f"Wb{l}"] = _np.ascontiguousarray(inp["w_branch"][l].reshape(3 * 2048, D_MODEL), dtype=_np.float32)
        shared[f"Wo{l}"] = _np.ascontiguousarray(inp["w_out"][l], dtype=_np.float32)
    per = [[layer_consts(inp, l, g) for g in range(4)] for l in range(DEPTH)]
    for k in FUSED_CONSTS:
        if k in PER_LAYER and k in PER_GROUP:
            shared[k] = _np.stack([_np.stack([per[l][g][k] for g in range(4)]) for l in range(DEPTH)])
        elif k in PER_LAYER:
            shared[k] = _np.stack([per[l][0][k] for l in range(DEPTH)])
        elif k in PER_GROUP:
            shared[k] = _np.stack([per[0][g][k] for g in range(4)])
        else:
            shared[k] = per[0][0][k]
        shared[k] = _np.ascontiguousarray(shared[k], dtype=_np.float32)
    xTb = [_np.ascontiguousarray(x[b].T) for b in range(BATCH)]
    in_maps = []
    for c in cores:
        m = dict(shared)
        m["xT"] = xTb[c // 4]
        in_maps.append(m)
    res = run_bass_kernel_spmd(_prog("fused", build_fused), in_maps, core_ids=cores).results
    out = _np.empty((BATCH, SEQ, D_MODEL), _np.float32)
    for c in cores:
        b, g = c // 4, c % 4
        out[b, g * TQ:(g + 1) * TQ, :] = res[c]["xo"][:, g * TQ:(g + 1) * TQ].T
    return out


FUSED = True
_kernel_unfused = kernel


def kernel(**inp):
    if FUSED:
        return kernel_fused({k: _np.asarray(v) for k, v in inp.items()})
    return _kernel_unfused(**inp)
```
